# Optimizing a Trainium2 kernel written in Bass

```python
import jax, jax.numpy as jnp
from jax import lax
import numpy as np

D_MODEL = 2048
BATCH = 8
SEQ = 2048
DEPTH = 1

CTX_LEN = 256
GRID_W = 64
D_MIX = D_MODEL
DN_HEADS = 8
DN_HEAD_DIM = D_MIX // 2 // DN_HEADS
DN_WIDTH = DN_HEADS * DN_HEAD_DIM
DN_CHUNK = 64
CONV_K = 5
MLP_WIDTH = D_MIX - DN_WIDTH
MLP_GROUPS = 8
MLP_GROUP_DIM = MLP_WIDTH // MLP_GROUPS
MLP_CHUNK = 128
CHUNK_ROWS = MLP_CHUNK // GRID_W
D_FF = ((8 * D_MODEL // 3 + 127) // 128) * 128
N_MOD = 9
IN_COLS = 4 * DN_WIDTH + 4 * DN_HEADS + 2 * MLP_WIDTH
EPS = 1e-6

kernel_name = "hybrid_deltanet_chunkmlp_macaron_dit"


def _rmsnorm(x, g):
    xf = x.astype(jnp.float32)
    y = xf * lax.rsqrt(jnp.mean(xf * xf, axis=-1, keepdims=True) + EPS)
    return (y * g.astype(jnp.float32)).astype(x.dtype)


def _l2norm(x):
    xf = x.astype(jnp.float32)
    return xf * lax.rsqrt(jnp.sum(xf * xf, axis=-1, keepdims=True) + EPS)


def _modulate(h, shift, scale):
    return h * (1.0 + scale) + shift


def _swiglu(h, w_in, w_out):
    a, b = jnp.split(h @ w_in, 2, axis=-1)
    return (jax.nn.silu(a) * b) @ w_out


def _split_proj(z):
    sizes = [DN_WIDTH] * 4 + [2 * DN_HEADS] * 2 + [MLP_WIDTH] * 2
    idx = [int(i) for i in np.cumsum(sizes)[:-1]]
    q, k, v, gate, a, b, u, vm = jnp.split(z, idx, axis=-1)
    return jnp.concatenate([q, k, v], axis=-1), gate, a, b, u, vm


def _short_conv(x, w):
    C = x.shape[-1]
    y = lax.conv_general_dilated(
        x, w[:, None, :].astype(x.dtype), window_strides=(1,),
        padding=[(CONV_K // 2, CONV_K // 2)],
        dimension_numbers=('NWC', 'WIO', 'NWC'), feature_group_count=C)
    return jax.nn.silu(y)


def _dn_prepare(z_qkv, z_a, z_b, conv_w, a_log, dt_bias):
    B, T, _ = z_qkv.shape
    qkv = _short_conv(z_qkv, conv_w)
    q, k, v = (t.reshape(B, T, DN_HEADS, DN_HEAD_DIM) for t in jnp.split(qkv, 3, axis=-1))
    q = _l2norm(q) * (DN_HEAD_DIM ** -0.5)
    k = _l2norm(k)
    a = z_a.astype(jnp.float32).reshape(B, T, 2, DN_HEADS)
    g = -jnp.exp(a_log.astype(jnp.float32)) * jax.nn.softplus(a + dt_bias.astype(jnp.float32))
    beta = jax.nn.sigmoid(z_b.astype(jnp.float32).reshape(B, T, 2, DN_HEADS))
    return q, k, v, g, beta


def _chunk_gated_delta(q, k, v, g, beta, state0):
    f32 = jnp.float32
    q, k, v, g, beta = (t.astype(f32) for t in (q, k, v, g, beta))
    B, T, H, _ = q.shape
    Dv = v.shape[-1]
    C = DN_CHUNK
    N = T // C

    def chunks(t):
        return jnp.swapaxes(t.reshape((B, N, C, H) + t.shape[3:]), 2, 3)

    qc, kc, vc, gc, bc = map(chunks, (q, k, v, g, beta))
    gc = jnp.cumsum(gc, axis=-1)
    tri = jnp.tril(jnp.ones((C, C), dtype=bool))
    strict = jnp.tril(jnp.ones((C, C), dtype=bool), -1)
    decay = jnp.exp(jnp.where(tri, gc[..., :, None] - gc[..., None, :], -jnp.inf))
    kb = kc * bc[..., None]
    L = jnp.where(strict, jnp.einsum('bnhid,bnhjd->bnhij', kb, kc) * decay, 0.0)
    eye = jnp.eye(C, dtype=f32)
    Tm = lax.linalg.triangular_solve(eye + L, jnp.broadcast_to(eye, L.shape),
                                     left_side=True, lower=True)
    u = jnp.einsum('bnhij,bnhje->bnhie', Tm, vc * bc[..., None])
    w = jnp.einsum('bnhij,bnhjd->bnhid', Tm, kb * jnp.exp(gc)[..., None])
    qk = jnp.where(tri, jnp.einsum('bnhid,bnhjd->bnhij', qc, kc) * decay, 0.0)
    q_dec = qc * jnp.exp(gc)[..., None]
    k_dec = kc * jnp.exp(gc[..., -1:] - gc)[..., None]
    g_last = jnp.exp(gc[..., -1])

    def step(S, xs):
        qk_i, qd_i, w_i, u_i, kd_i, gl_i = xs
        v_new = u_i - jnp.einsum('bhcd,bhde->bhce', w_i, S)
        o_i = jnp.einsum('bhcd,bhde->bhce', qd_i, S) + jnp.einsum('bhij,bhje->bhie', qk_i, v_new)
        S = S * gl_i[..., None, None] + jnp.einsum('bhcd,bhce->bhde', kd_i, v_new)
        return S, o_i

    xs = tuple(jnp.moveaxis(t, 1, 0) for t in (qk, q_dec, w, u, k_dec, g_last))
    S, o = lax.scan(step, state0.astype(f32), xs)
    o = jnp.transpose(o, (1, 0, 3, 2, 4)).reshape(B, T, H, Dv)
    return o, S


def _bidir_delta(dn_ctx, dn_lat):
    qc, kc, vc, gc, bc = dn_ctx
    ql, kl, vl, gl, bl = dn_lat
    B = ql.shape[0]
    o_ctx, o_lat = 0.0, 0.0
    for d in range(2):
        f = (lambda t: jnp.flip(t, axis=1)) if d == 1 else (lambda t: t)
        s0 = jnp.zeros((B, DN_HEADS, DN_HEAD_DIM, DN_HEAD_DIM), jnp.float32)
        oc, s_ctx = _chunk_gated_delta(f(qc), f(kc), f(vc), f(gc[:, :, d]), f(bc[:, :, d]), s0)
        ol, _ = _chunk_gated_delta(f(ql), f(kl), f(vl), f(gl[:, :, d]), f(bl[:, :, d]), s_ctx)
        o_ctx = o_ctx + f(oc)
        o_lat = o_lat + f(ol)
    return o_ctx, o_lat


def _gated_head_norm(o, z_gate, g):
    B, T = o.shape[:2]
    zg = z_gate.astype(jnp.float32).reshape(B, T, DN_HEADS, DN_HEAD_DIM)
    y = _rmsnorm(o, g) * jax.nn.silu(zg)
    return y.reshape(B, T, DN_WIDTH).astype(z_gate.dtype)


def _chunk_mlp(z_u, z_v, n_chunks, w_s, b_s, v_g):
    B, T, _ = z_u.shape
    u = jax.nn.gelu(z_u)
    v = jax.nn.gelu(z_v).reshape(B, n_chunks, MLP_CHUNK, MLP_GROUPS, MLP_GROUP_DIM)
    v = _rmsnorm(v, v_g.reshape(MLP_GROUPS, MLP_GROUP_DIM))
    s = jnp.einsum('gpq,bnqgc->bnpgc', w_s, v) + b_s.T[None, None, :, :, None]
    return u * s.reshape(B, T, MLP_WIDTH)


def setup_inputs(seed: int = 0) -> dict:
    key = jax.random.key(seed)
    ks = jax.random.split(key, 24)
    nrm = jax.random.normal
    f32 = jnp.float32
    dt = jnp.exp(jax.random.uniform(ks[12], (DEPTH, 2, DN_HEADS), f32,
                                    float(np.log(1e-3)), float(np.log(1e-1))))
    return {
        'x': nrm(ks[0], (BATCH, SEQ, D_MODEL), f32),
        'c': nrm(ks[1], (BATCH, D_MODEL), f32),
        'ctx': nrm(ks[2], (BATCH, CTX_LEN, D_MODEL), f32),
        'c_ctx': nrm(ks[3], (D_MODEL,), f32),
        'w_mod': nrm(ks[4], (DEPTH, D_MODEL, N_MOD * D_MODEL), f32) * (0.5 * D_MODEL ** -0.5),
        'b_mod': nrm(ks[5], (DEPTH, N_MOD * D_MODEL), f32) * 0.01,
        'norm_g': 1.0 + 0.1 * nrm(ks[6], (DEPTH, 3, D_MODEL), f32),
        'ffn1_w_in': nrm(ks[7], (DEPTH, D_MODEL, 2 * D_FF), f32) * D_MODEL ** -0.5,
        'ffn1_w_out': nrm(ks[8], (DEPTH, D_FF, D_MODEL), f32) * D_FF ** -0.5,
        'w_in': nrm(ks[9], (DEPTH, D_MODEL, IN_COLS), f32) * D_MODEL ** -0.5,
        'conv_w': nrm(ks[10], (DEPTH, CONV_K, 3 * DN_WIDTH), f32) * CONV_K ** -0.5,
        'a_log': jnp.log(jax.random.uniform(ks[11], (DEPTH, 2, DN_HEADS), f32, 1.0, 16.0)),
        'dt_bias': dt + jnp.log(-jnp.expm1(-dt)),
        'head_norm_g': 1.0 + 0.1 * nrm(ks[13], (DEPTH, DN_HEAD_DIM), f32),
        'spatial_w': nrm(ks[14], (DEPTH, MLP_GROUPS, MLP_CHUNK, MLP_CHUNK), f32) * MLP_CHUNK ** -0.5,
        'spatial_b': 1.0 + 0.1 * nrm(ks[15], (DEPTH, MLP_GROUPS, MLP_CHUNK), f32),
        'mlp_norm_g': 1.0 + 0.1 * nrm(ks[16], (DEPTH, MLP_WIDTH), f32),
        'w_out': nrm(ks[17], (DEPTH, D_MIX, D_MODEL), f32) * D_MIX ** -0.5,
        'ffn2_w_in': nrm(ks[18], (DEPTH, D_MODEL, 2 * D_FF), f32) * D_MODEL ** -0.5,
        'ffn2_w_out': nrm(ks[19], (DEPTH, D_FF, D_MODEL), f32) * D_FF ** -0.5,
        'final_g': 1.0 + 0.1 * nrm(ks[20], (D_MODEL,), f32),
    }


def reference(x, c, ctx, c_ctx, w_mod, b_mod, norm_g, ffn1_w_in, ffn1_w_out, w_in, conv_w,
              a_log, dt_bias, head_norm_g, spatial_w, spatial_b, mlp_norm_g, w_out,
              ffn2_w_in, ffn2_w_out, final_g):
    rows = x.shape[1] // GRID_W
    n_chunks_lat = rows // CHUNK_ROWS
    n_chunks_ctx = ctx.shape[1] // MLP_CHUNK
    for l in range(DEPTH):
        last = l == DEPTH - 1
        m = jnp.split((jax.nn.silu(c) @ w_mod[l] + b_mod[l])[:, None, :], N_MOD, axis=-1)
        mc = jnp.split((jax.nn.silu(c_ctx) @ w_mod[l] + b_mod[l])[None, None, :], N_MOD, axis=-1)

        x = x + 0.5 * m[2] * _swiglu(_modulate(_rmsnorm(x, norm_g[l, 0]), m[0], m[1]),
                                     ffn1_w_in[l], ffn1_w_out[l])
        ctx = ctx + 0.5 * mc[2] * _swiglu(_modulate(_rmsnorm(ctx, norm_g[l, 0]), mc[0], mc[1]),
                                          ffn1_w_in[l], ffn1_w_out[l])

        h_x = _modulate(_rmsnorm(x, norm_g[l, 1]), m[3], m[4])
        h_c = _modulate(_rmsnorm(ctx, norm_g[l, 1]), mc[3], mc[4])
        qkv_x, gate_x, a_x, b_x, u_x, v_x = _split_proj(h_x @ w_in[l])
        qkv_c, gate_c, a_c, b_c, u_c, v_c = _split_proj(h_c @ w_in[l])

        dn_x = _dn_prepare(qkv_x, a_x, b_x, conv_w[l], a_log[l], dt_bias[l])
        dn_c = _dn_prepare(qkv_c, a_c, b_c, conv_w[l], a_log[l], dt_bias[l])
        o_c_dn, o_x_dn = _bidir_delta(dn_c, dn_x)
        o_x_a = _gated_head_norm(o_x_dn, gate_x, head_norm_g[l])
        o_x_b = _chunk_mlp(u_x, v_x, n_chunks_lat, spatial_w[l], spatial_b[l], mlp_norm_g[l])
        x = x + m[5] * (jnp.concatenate([o_x_a, o_x_b], axis=-1) @ w_out[l])

        if not last:
            o_c_a = _gated_head_norm(o_c_dn, gate_c, head_norm_g[l])
            o_c_b = _chunk_mlp(u_c, v_c, n_chunks_ctx, spatial_w[l], spatial_b[l], mlp_norm_g[l])
            ctx = ctx + mc[5] * (jnp.concatenate([o_c_a, o_c_b], axis=-1) @ w_out[l])
            ctx = ctx + 0.5 * mc[8] * _swiglu(_modulate(_rmsnorm(ctx, norm_g[l, 2]), mc[6], mc[7]),
                                              ffn2_w_in[l], ffn2_w_out[l])

        x = x + 0.5 * m[8] * _swiglu(_modulate(_rmsnorm(x, norm_g[l, 2]), m[6], m[7]),
                                     ffn2_w_in[l], ffn2_w_out[l])
    return _rmsnorm(x, final_g)
```

```python
import numpy as np
from contextlib import ExitStack
import concourse.bass as bass
import concourse.mybir as mybir
from concourse.bass_utils import run_bass_kernel_spmd

F32 = mybir.dt.float32
BF16 = mybir.dt.bfloat16
AF = mybir.ActivationFunctionType
ALU = mybir.AluOpType

P = 128
D = 2048
KT = 16
T_LAT = 2048
T_CTX = 256
NTOK = T_LAT + T_CTX
NTT = NTOK // P
DFF = 5504
FT = DFF // P
NMOD = 9
EPS = 1e-6
IN_COLS = 6176
NH = 8
ENGS = ("pe", "act", "dve", "pool", "sp")
STAGE = 99
DEBUG = False
PAD_ROWS = 768


class Op:
    __slots__ = ("eng", "fn", "deps", "idx", "sig", "signo", "dma", "dsem", "dval", "prev_dval")

    def __init__(self, eng, fn, dma):
        self.eng = eng
        self.fn = fn
        self.dma = dma
        self.deps = set()
        self.sig = False
        self.signo = 0
        self.dsem = None
        self.dval = 0
        self.prev_dval = 0


class Sched:
    EP = 2048
    NDMA = {"sp": 20, "act": 6, "pool": 20}

    def __init__(self, nc, es):
        self.nc = nc
        self.es = es
        self.q = {e: [] for e in ENGS}
        self.lastw = {}
        self.readers = {}
        self.cnt = {e: 0 for e in ENGS}
        self.csem = {e: [] for e in ENGS}
        self.dsem = {e: [es.enter_context(nc.semaphore(f"d_{e}_{i}")) for i in range(n)]
                     for e, n in self.NDMA.items()}
        self.dcum = {e: [0] * n for e, n in self.NDMA.items()}
        self.dnext = {e: 0 for e in self.NDMA}
        self.seen = {}
        self.seen_d = {}
        self.bar = []
        self.bar_pending = set()
        self.last_op = {e: None for e in ENGS}
        self.phase_dmas = []

    def add(self, eng, fn, reads=(), writes=(), dma=False, deps=()):
        op = Op(eng, fn, dma)
        d = set(x for x in deps if x is not None)
        for k in reads:
            w = self.lastw.get(k)
            if w is not None:
                d.add(w)
        for k in writes:
            w = self.lastw.get(k)
            if w is not None:
                d.add(w)
            d.update(self.readers.get(k, ()))
        for k in reads:
            self.readers.setdefault(k, []).append(op)
        for k in writes:
            self.lastw[k] = op
            self.readers[k] = []
        if eng in self.bar_pending:
            d.update(self.bar)
            self.bar_pending.discard(eng)
        d.discard(op)
        op.deps = d
        for x in d:
            x.sig = True
        self.q[eng].append(op)
        self.last_op[eng] = op
        if dma:
            self.phase_dmas.append(op)
        return op

    def barrier(self):
        self.bar = [o for o in self.last_op.values() if o is not None] + list(self.phase_dmas)
        for o in self.bar:
            o.sig = True
        self.bar_pending = set(ENGS)
        self.lastw = {}
        self.readers = {}

    def _sem_for(self, eng, n):
        ep = (n - 1) // self.EP
        while len(self.csem[eng]) <= ep:
            self.csem[eng].append(self.es.enter_context(
                self.nc.semaphore(f"c_{eng}_{len(self.csem[eng])}")))
        return self.csem[eng][ep], (n - 1) % self.EP + 1

    def emit(self):
        nc = self.nc
        for e in ENGS:
            for op in self.q[e]:
                if op.dma:
                    i = self.dnext[e]
                    self.dnext[e] = (i + 1) % len(self.dsem[e])
                    op.dsem = (e, i)
                    op.prev_dval = self.dcum[e][i]
                    self.dcum[e][i] += 16
                    op.dval = self.dcum[e][i]
                elif op.sig:
                    self.cnt[e] += 1
                    op.signo = self.cnt[e]
        plans = {}
        for e in ENGS:
            plan = []
            for op in self.q[e]:
                need = {}
                dneed = {}
                for dd in op.deps:
                    if dd.dma:
                        dneed[dd.dsem] = max(dneed.get(dd.dsem, 0), dd.dval)
                    else:
                        if dd.eng == e and e == "pe":
                            continue
                        need[dd.eng] = max(need.get(dd.eng, 0), dd.signo)
                if op.dma and op.prev_dval > 0:
                    dneed[op.dsem] = max(dneed.get(op.dsem, 0), op.prev_dval)
                waits = []
                for pe_, n in need.items():
                    if n > self.seen.get((e, pe_), 0):
                        self.seen[(e, pe_)] = n
                        waits.append(self._sem_for(pe_, n))
                for ds, v in dneed.items():
                    if v > self.seen_d.get((e, ds), 0):
                        self.seen_d[(e, ds)] = v
                        waits.append((self.dsem[ds[0]][ds[1]], v))
                inc = None
                if op.dma:
                    inc = (self.dsem[op.dsem[0]][op.dsem[1]], 16)
                elif op.sig:
                    inc = (self._sem_for(e, op.signo)[0], 1)
                plan.append((waits, op.fn, inc))
            plans[e] = plan

        def run(engine, plan):
            for waits, fn, inc in plan:
                for s, v in waits:
                    engine.wait_ge(s, v)
                ins = fn(engine)
                if inc is not None:
                    ins.then_inc(inc[0], inc[1])

        with nc.Block() as block:
            @block.tensor
            def _(e):
                run(e, plans["pe"])

            @block.scalar
            def _(e):
                run(e, plans["act"])

            @block.vector
            def _(e):
                run(e, plans["dve"])

            @block.gpsimd
            def _(e):
                run(e, plans["pool"])

            @block.sync
            def _(e):
                run(e, plans["sp"])
        self.q = {e: [] for e in ENGS}

    def end_phase(self):
        self.barrier()
        self.emit()
        self.phase_dmas = []


def build_program():
    nc = bass.Bass("TRN2", target_bir_lowering=False)
    dt_in = {}

    def din(name, shape):
        t = nc.dram_tensor(name, list(shape), F32, kind="ExternalInput")
        dt_in[name] = t
        return t.ap()

    x_d = din("x", [T_LAT, D])
    ctx_d = din("ctx", [T_CTX, D])
    c2_d = din("c2", [32, P])
    wmod_d = din("w_mod", [D, NMOD * D])
    bmod_d = din("b_mod", [144, P])
    ng_d = din("norm_g", [48, P])
    f1in_d = din("ffn1_w_in", [D, 2 * DFF])
    f1out_d = din("ffn1_w_out", [DFF, D])
    win_d = din("w_in", [D, IN_COLS])
    convw_d = din("conv_w", [120, P])
    alog_d = din("a_log", [1, 16])
    dtb_d = din("dt_bias", [1, 16])
    hng_d = din("head_norm_g", [1, P])
    spw_d = din("spatial_w", [NH, P, P])
    spb_d = din("spatial_b", [1, NH * P])
    mng_d = din("mlp_norm_g", [8, P])
    wout_d = din("w_out", [D, D])
    f2in_d = din("ffn2_w_in", [D, 2 * DFF])
    f2out_d = din("ffn2_w_out", [DFF, D])
    fg_d = din("final_g", [16, P])
    mkf_d = din("masks_f", [5, P, P])
    mkb_d = din("masks_b", [16, P, P])
    out_d = nc.dram_tensor("out", [T_LAT, D], F32, kind="ExternalOutput").ap()

    skind = "ExternalOutput" if DEBUG else "Internal"
    PADT = nc.dram_tensor("padscr", [PAD_ROWS, D], F32, kind="Internal").ap()

    def scr(name, shape, dt):
        return nc.dram_tensor(name, list(shape), dt, kind=skind).ap()
    XT0 = scr("XT0", [D, NTOK], F32)
    XT1 = scr("XT1", [D, NTOK], F32)
    XT2 = scr("XT2", [D, NTOK], F32)
    XT3 = scr("XT3", [D, NTOK], F32)
    QTs = scr("QT", [NH * P, NTOK], BF16)
    KTs = scr("KT", [NH * P, NTOK], BF16)
    KTOK = scr("KTOK", [NTOK, NH * P], BF16)
    VTOK = scr("VTOK", [NTOK, NH * P], BF16)
    SGT = scr("SGT", [NH * P, T_LAT], BF16)
    OB = scr("OB", [NH * P, T_LAT], BF16)
    OACC = scr("OACC", [NH * P, T_LAT], BF16) if DEBUG else None

    es = ExitStack()
    with es:
        S = Sched(nc, es)
        psg = [es.enter_context(nc.psum_tensor(f"psg{i}", [P, 1024], F32)) for i in range(4)]
        ps = [psg[i // 2][:, (i % 2) * 512:(i % 2 + 1) * 512] for i in range(8)]
        ident_f = es.enter_context(nc.sbuf_tensor("ident_f", [P, P], F32))
        ident_b = es.enter_context(nc.sbuf_tensor("ident_b", [P, P], BF16))
        ones_b = es.enter_context(nc.sbuf_tensor("ones_b", [P, P], BF16))
        sc = es.enter_context(nc.sbuf_tensor("sc", [P, KT, 2], BF16))
        bm = es.enter_context(nc.sbuf_tensor("bm", [P, 144], F32))
        ng = es.enter_context(nc.sbuf_tensor("ng", [P, 48], F32))
        fg = es.enter_context(nc.sbuf_tensor("fg", [P, 16], F32))
        modv = es.enter_context(nc.sbuf_tensor("modv", [P, 144, 2], F32))
        gs = es.enter_context(nc.sbuf_tensor("gs", [P, 3, KT, 2], F32))
        sh = es.enter_context(nc.sbuf_tensor("sh", [P, 3, KT, 2], F32))
        gate = es.enter_context(nc.sbuf_tensor("gate", [P, 3, KT, 2], F32))
        cw = es.enter_context(nc.sbuf_tensor("cw", [P, 120], F32))
        mg = es.enter_context(nc.sbuf_tensor("mg", [P, 8], F32))
        hg = es.enter_context(nc.sbuf_tensor("hg", [P, 1], F32))
        WsT = es.enter_context(nc.sbuf_tensor("WsT", [P, NH, P], BF16))
        sb_bc = es.enter_context(nc.sbuf_tensor("sb_bc", [P, NH * P], F32))
        ab_tok = es.enter_context(nc.sbuf_tensor("ab_tok", [P, NTT, 32], F32))

        def k_(name, *idx):
            return (name,) + idx

        S.add("pool", lambda e: e.memset(ident_f[:], 0.0), writes=[k_("ident_f")])
        S.add("pool", lambda e: e.affine_select(out=ident_f[:], in_=ident_f[:], pattern=[[-1, P]],
                                                compare_op=ALU.not_equal, fill=1.0, base=0, channel_multiplier=1),
              reads=[k_("ident_f")], writes=[k_("ident_f")])
        S.add("pool", lambda e: e.memset(ones_b[:], 1.0), writes=[k_("ones_b")])
        S.add("pool", lambda e: e.tensor_copy(out=ident_b[:], in_=ident_f[:]), reads=[k_("ident_f"), k_("ones_b")],
              writes=[k_("ident")])

        S.add("sp", lambda e: e.dma_start(out=PADT[0:P, 0:P], in_=ident_f[:]), dma=True, reads=[k_("ident")])
        with ExitStack() as pes:
            stage = pes.enter_context(nc.sbuf_tensor("pstage", [P, 4, P], F32))
            stage2 = pes.enter_context(nc.sbuf_tensor("pstage2", [P, 4, P], F32))
            c2raw = pes.enter_context(nc.sbuf_tensor("c2raw", [P, 32], F32))

            def load_T(idx, src_ap, rows, dst_ap, bank, post=None):
                st = stage if idx < 4 else stage2
                sl = idx % 4
                S.add("sp", lambda e: e.dma_start(out=st[:rows, sl, :], in_=src_ap), dma=True,
                      writes=[k_("pst", idx)])
                S.add("pe", lambda e: e.transpose(out=ps[bank][:, :rows], in_=st[:rows, sl, :],
                                                  identity=ident_f[:rows, :rows]),
                      reads=[k_("pst", idx), k_("ident")], writes=[k_("ps", bank)])
                S.add("dve", lambda e: e.tensor_copy(out=dst_ap, in_=ps[bank][:, :rows]),
                      reads=[k_("ps", bank)], writes=[k_("pc", idx)])

            load_T(0, c2_d, 32, c2raw[:], 0)
            load_T(1, bmod_d[0:128, :], 128, bm[:, 0:128], 1)
            load_T(2, bmod_d[128:144, :], 16, bm[:, 128:144], 2)
            load_T(3, ng_d, 48, ng[:], 3)
            load_T(4, fg_d, 16, fg[:], 4)
            load_T(5, convw_d, 120, cw[:], 5)
            load_T(6, mng_d, 8, mg[:], 6)
            S.add("sp", lambda e: e.dma_start(out=hg[:], in_=hng_d.rearrange("o p -> p o")), dma=True, writes=[k_("hg")])
            S.add("sp", lambda e: e.dma_start(out=sb_bc[:], in_=spb_d.partition_broadcast(P)), dma=True, writes=[k_("sb_bc")])
            spst = pes.enter_context(nc.sbuf_tensor("spst", [P, NH, P], F32))
            S.add("sp", lambda e: e.dma_start(out=spst[:], in_=spw_d.rearrange("g p q -> p g q")), dma=True, writes=[k_("spst")])
            for half in range(2):
                bank = 5 + half

                def trw(e, half=half, bank=bank):
                    ins = None
                    for j in range(4):
                        g_ = half * 4 + j
                        ins = e.transpose(out=ps[bank][:, j * P:(j + 1) * P], in_=spst[:, g_, :], identity=ident_f[:])
                    return ins
                S.add("pe", trw, reads=[k_("spst"), k_("ident")], writes=[k_("ps", bank)])
                S.add("dve", lambda e, half=half, bank=bank: e.tensor_copy(
                    out=WsT[:, half * 4:(half + 1) * 4, :], in_=ps[bank][:].rearrange("p (a j) -> p a j", a=4)),
                    reads=[k_("ps", bank)], writes=[k_("WsT", half)])
            S.add("act", lambda e: e.activation(out=sc[:].rearrange("p k r -> p r k"),
                                                in_=c2raw[:].rearrange("p (r k) -> p r k", r=2),
                                                func=AF.Silu),
                  reads=[k_("pc", 0)], writes=[k_("sc")])

            NSL = 3
            wsl = [pes.enter_context(nc.sbuf_tensor(f"wmsl{i}", [P, KT, 512], BF16)) for i in range(NSL)]
            wmod_v = wmod_d.rearrange("(k p) n -> p k n", p=P)
            MB = 7
            XT0v = XT0.rearrange("(k p) t -> p k t", p=P)
            xin = [pes.enter_context(nc.sbuf_tensor(f"xin{i}", [P, D], F32)) for i in range(2)]
            xo = [pes.enter_context(nc.sbuf_tensor(f"xo{i}", [P, KT, P], F32)) for i in range(2)]
            tbank = [0]

            def emit_T(tt):
                sl = tt % 2
                src = ctx_d[tt * P:(tt + 1) * P, :] if tt < 2 else x_d[(tt - 2) * P:(tt - 1) * P, :]
                S.add("sp", lambda e: e.dma_start(out=xin[sl][:], in_=src), dma=True, writes=[k_("xin", sl)])
                for q4 in range(4):
                    bank = tbank[0] % 7
                    tbank[0] += 1

                    def tr(e, q4=q4, bank=bank):
                        ins = None
                        for j in range(4):
                            k = q4 * 4 + j
                            ins = e.transpose(out=ps[bank][:, j * P:(j + 1) * P], in_=xin[sl][:, k * P:(k + 1) * P],
                                              identity=ident_f[:])
                        return ins
                    S.add("pe", tr, reads=[k_("xin", sl), k_("ident")], writes=[k_("ps", bank)])
                    dst = xo[sl][:, q4 * 4:(q4 + 1) * 4, :]
                    srcp = ps[bank][:].rearrange("p (j t) -> p j t", j=4)
                    if q4 % 2 == 0:
                        S.add("dve", lambda e, dst=dst, srcp=srcp: e.tensor_copy(out=dst, in_=srcp),
                              reads=[k_("ps", bank)], writes=[k_("xo", sl, q4)])
                    else:
                        S.add("act", lambda e, dst=dst, srcp=srcp: e.activation(out=dst, in_=srcp, func=AF.Copy),
                              reads=[k_("ps", bank)], writes=[k_("xo", sl, q4)])
                S.add("sp", lambda e: e.dma_start(out=XT0v[:, :, tt * P:(tt + 1) * P], in_=xo[sl][:]),
                      dma=True, reads=[k_("xo", sl, q) for q in range(4)])

            for nb in range(36):
                sl = nb % NSL
                S.add("pool", lambda e, nb=nb, sl=sl: e.dma_start(
                    out=wsl[sl][:], in_=wmod_v[:, :, nb * 512:(nb + 1) * 512]),
                    dma=True, writes=[k_("wm", sl)])

                def mm(e, nb=nb, sl=sl):
                    ins = None
                    for c in range(4):
                        n = nb * 4 + c
                        for k in range(KT):
                            ins = e.matmul(ps[MB][:, 2 * n:2 * n + 2], lhsT=wsl[sl][:, k, c * P:(c + 1) * P],
                                           rhs=sc[:, k, :], start=(k == 0), stop=(k == KT - 1))
                    return ins
                S.add("pe", mm, reads=[k_("wm", sl), k_("sc")], writes=[k_("ps", MB)])
                if nb % 2 == 1:
                    emit_T(nb // 2)
            psm = ps[MB][:, 0:288].rearrange("p (n r) -> p n r", r=2)

            def evac_mod(e):
                ins = None
                for r in range(2):
                    ins = e.tensor_tensor(out=modv[:, :, r], in0=psm[:, :, r], in1=bm[:], op=ALU.add)
                return ins
            S.add("dve", evac_mod, reads=[k_("ps", MB), k_("pc", 1), k_("pc", 2)], writes=[k_("modv")])

            def derive(e):
                ins = None
                for s in range(3):
                    for r in range(2):
                        e.scalar_tensor_tensor(out=gs[:, s, :, r], in0=modv[:, (3 * s + 1) * 16:(3 * s + 2) * 16, r],
                                               scalar=1.0, in1=ng[:, s * 16:(s + 1) * 16],
                                               op0=ALU.add, op1=ALU.mult)
                        e.tensor_copy(out=sh[:, s, :, r], in_=modv[:, (3 * s) * 16:(3 * s + 1) * 16, r])
                        ins = e.tensor_scalar(out=gate[:, s, :, r],
                                              in0=modv[:, (3 * s + 2) * 16:(3 * s + 3) * 16, r],
                                              scalar1=(1.0 if s == 1 else 0.5), scalar2=None, op0=ALU.mult)
                return ins
            S.add("dve", derive, reads=[k_("modv"), k_("pc", 3)], writes=[k_("mods")])
            S.end_phase()

        XT0v = XT0.rearrange("(k p) t -> p k t", p=P)
        XT1v = XT1.rearrange("(k p) t -> p k t", p=P)
        XT2v = XT2.rearrange("(k p) t -> p k t", p=P)
        XT3v = XT3.rearrange("(k p) t -> p k t", p=P)

        def ffn_phase(tag, XSv, XDv, w_in_ap, w_out_ap, s, blocks):
            w_in_v = w_in_ap.rearrange("(k p) n -> p k n", p=P)
            w_out_v = w_out_ap.rearrange("(j p) n -> p j n", p=P)
            JH = [(0, 22), (22, FT)]
            NSQ, NTMP = 8, 4
            with ExitStack() as pes:
                xT = pes.enter_context(nc.sbuf_tensor(f"{tag}xT", [P, KT, 512], F32))
                hT = pes.enter_context(nc.sbuf_tensor(f"{tag}hT", [P, KT, 512], BF16))
                gT = pes.enter_context(nc.sbuf_tensor(f"{tag}gT", [P, FT, 512], BF16))
                wA = [pes.enter_context(nc.sbuf_tensor(f"{tag}wA{i}", [P, KT, 2, 256], BF16)) for i in range(2)]
                wO = [pes.enter_context(nc.sbuf_tensor(f"{tag}wO{i}", [P, 22, 256], BF16)) for i in range(3)]
                sq = [pes.enter_context(nc.sbuf_tensor(f"{tag}sq{i}", [P, 512], BF16)) for i in range(NSQ)]
                lnv = pes.enter_context(nc.sbuf_tensor(f"{tag}lnv", [P, 512], F32))
                rstd = pes.enter_context(nc.sbuf_tensor(f"{tag}rstd", [P, 512], F32))
                tmp = [pes.enter_context(nc.sbuf_tensor(f"{tag}tmp{i}", [P, 512], F32)) for i in range(NTMP)]
                sa = [pes.enter_context(nc.sbuf_tensor(f"{tag}sa{i}", [P, 512], F32)) for i in range(2)]
                xr = [pes.enter_context(nc.sbuf_tensor(f"{tag}xr{i}", [P, 512], F32)) for i in range(4)]
                SB = 0
                ctr = dict(wa=0, wo=0, ab=0, y=0, xr=0)

                def e_load(bi):
                    t0, n, is_ctx = blocks[bi]
                    S.add("sp", lambda e: e.dma_start(out=xT[:, :, :n], in_=XSv[:, :, t0:t0 + n]),
                          dma=True, writes=[k_("xT", k) for k in range(KT)])

                def e_stats(bi):
                    t0, n, is_ctx = blocks[bi]
                    for k in range(KT):
                        S.add("act", lambda e, k=k: e.activation(out=sq[k % NSQ][:, :n], in_=xT[:, k, :n], func=AF.Square),
                              reads=[k_("xT", k)], writes=[k_("sq", k % NSQ)])
                        S.add("pe", lambda e, k=k: e.matmul(ps[SB][:, :n], lhsT=ones_b[:], rhs=sq[k % NSQ][:, :n],
                                                            start=(k == 0), stop=(k == KT - 1)),
                              reads=[k_("sq", k % NSQ), k_("ident")], writes=[k_("ps", SB)])
                    S.add("act", lambda e: e.activation(out=lnv[:, :n], in_=ps[SB][:, :n], func=AF.Ln, scale=1.0 / D, bias=EPS),
                          reads=[k_("ps", SB)], writes=[k_("lnv")])
                    S.add("act", lambda e: e.activation(out=rstd[:, :n], in_=lnv[:, :n], func=AF.Exp, scale=-0.5),
                          reads=[k_("lnv")], writes=[k_("rstd")])

                def e_mod(bi):
                    t0, n, is_ctx = blocks[bi]
                    r = 1 if is_ctx else 0
                    for k in range(KT):
                        S.add("dve", lambda e, k=k: e.scalar_tensor_tensor(
                            out=tmp[k % NTMP][:, :n], in0=xT[:, k, :n], scalar=gs[:, s, k, r:r + 1],
                            in1=rstd[:, :n], op0=ALU.mult, op1=ALU.mult),
                            reads=[k_("xT", k), k_("rstd"), k_("mods")], writes=[k_("tmp", k % NTMP)])
                        S.add("act", lambda e, k=k: e.activation(
                            out=hT[:, k, :n], in_=tmp[k % NTMP][:, :n], func=AF.Identity, bias=sh[:, s, k, r:r + 1]),
                            reads=[k_("tmp", k % NTMP), k_("mods")], writes=[k_("hT", k)])

                def e_up(bi, mid=None):
                    t0, n, is_ctx = blocks[bi]
                    for j in range(FT):
                        if j == 24 and mid is not None:
                            mid()
                        jj = j % 2
                        if jj == 0:
                            ws = ctr["wa"] % 2
                            ctr["wa"] += 1
                            wcols = min(256, DFF - j * P)
                            S.add("pool", lambda e, ws=ws, j=j, wcols=wcols: e.dma_start(
                                out=wA[ws][:, :, 0, 0:wcols], in_=w_in_v[:, :, j * P:j * P + wcols]),
                                dma=True, writes=[k_("wAa", ws)])
                            S.add("pool", lambda e, ws=ws, j=j, wcols=wcols: e.dma_start(
                                out=wA[ws][:, :, 1, 0:wcols], in_=w_in_v[:, :, DFF + j * P:DFF + j * P + wcols]),
                                dma=True, writes=[k_("wAb", ws)])
                        pa = 1 + 2 * (ctr["ab"] % 2)
                        pb = pa + 1
                        ctr["ab"] += 1

                        def up(e, ws=ws, pa=pa, pb=pb, jj=jj):
                            ins = None
                            for k in range(KT):
                                e.matmul(ps[pa][:, :n], lhsT=wA[ws][:, k, 0, jj * P:(jj + 1) * P], rhs=hT[:, k, :n],
                                         start=(k == 0), stop=(k == KT - 1))
                            for k in range(KT):
                                ins = e.matmul(ps[pb][:, :n], lhsT=wA[ws][:, k, 1, jj * P:(jj + 1) * P], rhs=hT[:, k, :n],
                                               start=(k == 0), stop=(k == KT - 1))
                            return ins
                        S.add("pe", up, reads=[k_("wAa", ws), k_("wAb", ws)] + [k_("hT", k) for k in range(KT)],
                              writes=[k_("ps", pa), k_("ps", pb)])
                        ss = j % 2
                        S.add("act", lambda e, ss=ss, pa=pa: e.activation(out=sa[ss][:, :n], in_=ps[pa][:, :n], func=AF.Silu),
                              reads=[k_("ps", pa)], writes=[k_("sa", ss)])
                        S.add("dve", lambda e, ss=ss, pb=pb, j=j: e.tensor_tensor(
                            out=gT[:, j, :n], in0=sa[ss][:, :n], in1=ps[pb][:, :n], op=ALU.mult),
                            reads=[k_("sa", ss), k_("ps", pb)], writes=[k_("gT", j)])

                def e_down(bi):
                    t0, n, is_ctx = blocks[bi]
                    r = 1 if is_ctx else 0
                    for dpair in range(KT // 2):
                        yset = ctr["y"] % 2
                        ctr["y"] += 1
                        pys = (5, 6) if yset == 0 else (7, 3)
                        for jh, (j0, j1) in enumerate(JH):
                            ws = ctr["wo"] % 3
                            ctr["wo"] += 1
                            S.add("pool", lambda e, ws=ws, dpair=dpair, j0=j0, j1=j1: e.dma_start(
                                out=wO[ws][:, 0:j1 - j0, :], in_=w_out_v[:, j0:j1, dpair * 256:(dpair + 1) * 256]),
                                dma=True, writes=[k_("wO", ws)])

                            def down(e, ws=ws, pys=pys, j0=j0, j1=j1):
                                ins = None
                                for dt2 in range(2):
                                    for j in range(j0, j1):
                                        ins = e.matmul(ps[pys[dt2]][:, :n], lhsT=wO[ws][:, j - j0, dt2 * P:(dt2 + 1) * P],
                                                       rhs=gT[:, j, :n], start=(j == 0), stop=(j == FT - 1),
                                                       skip_group_check=True)
                                return ins
                            S.add("pe", down, reads=[k_("wO", ws)] + [k_("gT", j) for j in range(j0, j1)],
                                  writes=[k_("ps", pys[0]), k_("ps", pys[1])])
                        for dt2 in range(2):
                            dtile = dpair * 2 + dt2
                            xs = ctr["xr"] % 4
                            ctr["xr"] += 1
                            py = pys[dt2]
                            S.add("sp", lambda e, xs=xs, dtile=dtile: e.dma_start(out=xr[xs][:, :n], in_=XSv[:, dtile, t0:t0 + n]),
                                  dma=True, writes=[k_("xr", xs)])
                            S.add("dve", lambda e, py=py, dtile=dtile, xs=xs: e.scalar_tensor_tensor(
                                out=xr[xs][:, :n], in0=ps[py][:, :n], scalar=gate[:, s, dtile, r:r + 1],
                                in1=xr[xs][:, :n], op0=ALU.mult, op1=ALU.add),
                                reads=[k_("ps", py), k_("mods"), k_("xr", xs)], writes=[k_("xr", xs)])
                            S.add("sp", lambda e, xs=xs, dtile=dtile: e.dma_start(out=XDv[:, dtile, t0:t0 + n], in_=xr[xs][:, :n]),
                                  dma=True, reads=[k_("xr", xs)])

                nbk = len(blocks)
                e_load(0)
                e_stats(0)
                e_mod(0)
                for bi in range(nbk):
                    if bi + 1 < nbk:
                        e_load(bi + 1)
                        e_up(bi, mid=lambda bi=bi: e_stats(bi + 1))
                        e_mod(bi + 1)
                    else:
                        e_up(bi)
                    e_down(bi)
                S.end_phase()

        blocks_all = [(0, 256, True)] + [(256 + 512 * i, 512, False) for i in range(4)]
        blocks_lat = [(256 + 512 * i, 512, False) for i in range(4)]
        if STAGE >= 1:
            ffn_phase("f1", XT0v, XT1v, f1in_d, f1out_d, 0, blocks_all)
        XFv = XT1v if STAGE >= 1 else XT0v

        def zcol(t):
            return t + 2 if t < T_CTX else t + 6

        def z_phase():
            win_v = win_d.rearrange("(k p) n -> p k n", p=P)
            with ExitStack() as zes:
                h2T = zes.enter_context(nc.sbuf_tensor("h2T", [P, KT, NTOK], BF16))
                with ExitStack() as pes:
                    xT = [pes.enter_context(nc.sbuf_tensor(f"zxT{i}", [P, KT, 512], F32)) for i in range(2)]
                    NSQ, NTMP = 8, 4
                    sq = [pes.enter_context(nc.sbuf_tensor(f"zsq{i}", [P, 512], BF16)) for i in range(NSQ)]
                    lnv = [pes.enter_context(nc.sbuf_tensor(f"zlnv{i}", [P, 512], F32)) for i in range(2)]
                    rstd = [pes.enter_context(nc.sbuf_tensor(f"zrstd{i}", [P, 512], F32)) for i in range(2)]
                    tmp = [pes.enter_context(nc.sbuf_tensor(f"ztmp{i}", [P, 512], F32)) for i in range(NTMP)]
                    s = 1
                    nbz = len(blocks_all)

                    def z_load(bi):
                        t0, n, is_ctx = blocks_all[bi]
                        sl = bi % 2
                        S.add("sp", lambda e: e.dma_start(out=xT[sl][:, :, :n], in_=XT1v[:, :, t0:t0 + n]),
                              dma=True, writes=[k_("xT", sl, k) for k in range(KT)])

                    def z_stats(bi):
                        t0, n, is_ctx = blocks_all[bi]
                        sl = bi % 2
                        SB = bi % 2
                        for k in range(KT):
                            S.add("act", lambda e, k=k: e.activation(out=sq[k % NSQ][:, :n], in_=xT[sl][:, k, :n], func=AF.Square),
                                  reads=[k_("xT", sl, k)], writes=[k_("sq", k % NSQ)])
                            S.add("pe", lambda e, k=k: e.matmul(ps[SB][:, :n], lhsT=ones_b[:], rhs=sq[k % NSQ][:, :n],
                                                                start=(k == 0), stop=(k == KT - 1)),
                                  reads=[k_("sq", k % NSQ), k_("ident")], writes=[k_("ps", SB)])
                        S.add("act", lambda e: e.activation(out=lnv[sl][:, :n], in_=ps[SB][:, :n], func=AF.Ln, scale=1.0 / D, bias=EPS),
                              reads=[k_("ps", SB)], writes=[k_("lnv", sl)])
                        S.add("act", lambda e: e.activation(out=rstd[sl][:, :n], in_=lnv[sl][:, :n], func=AF.Exp, scale=-0.5),
                              reads=[k_("lnv", sl)], writes=[k_("rstd", sl)])

                    def z_mod(bi):
                        t0, n, is_ctx = blocks_all[bi]
                        r = 1 if is_ctx else 0
                        sl = bi % 2
                        for k in range(KT):
                            S.add("dve", lambda e, k=k: e.scalar_tensor_tensor(
                                out=tmp[k % NTMP][:, :n], in0=xT[sl][:, k, :n], scalar=gs[:, s, k, r:r + 1],
                                in1=rstd[sl][:, :n], op0=ALU.mult, op1=ALU.mult),
                                reads=[k_("xT", sl, k), k_("rstd", sl), k_("mods")], writes=[k_("tmp", k % NTMP)])
                            S.add("act", lambda e, k=k: e.activation(
                                out=h2T[:, k, t0:t0 + n], in_=tmp[k % NTMP][:, :n], func=AF.Identity, bias=sh[:, s, k, r:r + 1]),
                                reads=[k_("tmp", k % NTMP), k_("mods")], writes=[k_("h2T", k, bi)])

                    z_load(0)
                    z_load(1)
                    z_stats(0)
                    for bi in range(nbz):
                        if bi + 1 < nbz:
                            z_stats(bi + 1)
                        z_mod(bi)
                        if bi + 2 < nbz:
                            z_load(bi + 2)
                    S.end_phase()

                with ExitStack() as pes:
                    wz = [pes.enter_context(nc.sbuf_tensor(f"wz{i}", [P, KT, P], BF16)) for i in range(3)]
                    zrow = [pes.enter_context(nc.sbuf_tensor(f"zrow{i}", [P, NTOK + 8], BF16)) for i in range(2)]
                    crow = [pes.enter_context(nc.sbuf_tensor(f"crow{i}", [P, NTOK], F32)) for i in range(2)]
                    sqb = [pes.enter_context(nc.sbuf_tensor(f"sqb{i}", [P, 512], BF16)) for i in range(2)]
                    lnrow = [pes.enter_context(nc.sbuf_tensor(f"lnrow{i}", [P, NTOK], F32)) for i in range(2)]
                    rinv = [pes.enter_context(nc.sbuf_tensor(f"rinv{i}", [P, 512], F32)) for i in range(2)]
                    orow = [pes.enter_context(nc.sbuf_tensor(f"zorow{i}", [P, NTOK], BF16)) for i in range(2)]
                    tst = [pes.enter_context(nc.sbuf_tensor(f"tst{i}", [P, NTT, P], BF16)) for i in range(2)]
                    dgj = [pes.enter_context(nc.sbuf_tensor(f"dgj{i}", [P, 5, P], BF16)) for i in range(2)]
                    abT = pes.enter_context(nc.sbuf_tensor("abT", [32, NTOK], F32))
                    urow = [pes.enter_context(nc.sbuf_tensor(f"urow{i}", [P, T_LAT], BF16)) for i in range(2)]
                    obrow = [pes.enter_context(nc.sbuf_tensor(f"obrow{i}", [P, T_LAT], BF16)) for i in range(2)]
                    vmt = [pes.enter_context(nc.sbuf_tensor(f"vmt{i}", [P, 4, P], BF16)) for i in range(2)]
                    t1 = [pes.enter_context(nc.sbuf_tensor(f"t1{i}", [P, 512], F32)) for i in range(2)]
                    psb = [ps[6].bitcast(BF16), ps[7].bitcast(BF16)]

                    for zs in range(2):
                        S.add("pool", lambda e, zs=zs: e.memset(zrow[zs][:], 0.0),
                              writes=[k_("zrow", zs, b) for b in range(5)])

                    jobs = [("ab", 0)]
                    for h in range(NH):
                        jobs += [("q", h), ("k", h), ("v", h)]
                    jobs += [("g", h) for h in range(NH)]
                    for g_ in range(NH):
                        jobs += [("u", g_), ("vm", g_)]
                    cnt = dict(wz=0, z=0, zs=0, cs=0, c=0, st=0, sq=0, ri=0, os=0, tr=0, ts=0, dg=0, ob=0, vm=0, t1=0)

                    def nxt(name, mod):
                        v = cnt[name] % mod
                        cnt[name] += 1
                        return v

                    def job_gen(kind, h):
                        c0 = {"q": 0, "k": 1024, "v": 2048, "g": 3072, "ab": 4096, "u": 4128, "vm": 5152}[kind] + (0 if kind == "ab" else h * P)
                        w = 32 if kind == "ab" else P
                        ws = nxt("wz", 3)
                        S.add("pool", lambda e, ws=ws, c0=c0, w=w: e.dma_start(out=wz[ws][:, :, :w], in_=win_v[:, :, c0:c0 + w]),
                              dma=True, writes=[k_("wz", ws)])
                        blks = blocks_all if kind in ("ab", "q", "k", "v") else blocks_lat
                        conv = kind in ("q", "k", "v")
                        if conv:
                            zs = nxt("zs", 2)
                            ds = nxt("dg", 2)
                            ct = {"q": 0, "k": 8, "v": 16}[kind] + h

                            def mkdg(e, ds=ds, ct=ct):
                                ins = None
                                for tap in range(5):
                                    ins = e.tensor_scalar(out=dgj[ds][:, tap, :], in0=ident_b[:],
                                                          scalar1=cw[:, tap * 24 + ct:tap * 24 + ct + 1], scalar2=None,
                                                          op0=ALU.mult)
                                return ins
                            S.add("dve", mkdg, reads=[k_("ident"), k_("pc", 5)], writes=[k_("dgj", ds)])
                        if kind in ("q", "k", "vm"):
                            cs = nxt("cs", 2)
                        if kind != "ab" and kind != "u":
                            osl = nxt("os", 2)
                        for b, (t0, n, _) in enumerate(blks):
                            bidx = b if len(blks) == 5 else b + 1
                            pz = nxt("z", 2)

                            def mm(e, ws=ws, w=w, n=n, t0=t0, pz=pz):
                                ins = None
                                for k in range(KT):
                                    ins = e.matmul(ps[pz][:w, :n], lhsT=wz[ws][:, k, :w], rhs=h2T[:, k, t0:t0 + n],
                                                   start=(k == 0), stop=(k == KT - 1))
                                return ins
                            S.add("pe", mm, reads=[k_("wz", ws)] + [k_("h2T", k, bidx) for k in range(KT)],
                                  writes=[k_("ps", pz)])
                            tl = t0 - T_CTX
                            if conv:
                                S.add("dve", lambda e, zs=zs, t0=t0, n=n, pz=pz: e.tensor_copy(
                                    out=zrow[zs][:, zcol(t0):zcol(t0) + n], in_=ps[pz][:, :n]),
                                    reads=[k_("ps", pz)], writes=[k_("zrow", zs, bidx)])
                            elif kind == "ab":
                                S.add("dve", lambda e, t0=t0, n=n, pz=pz: e.tensor_copy(out=abT[:, t0:t0 + n], in_=ps[pz][:32, :n]),
                                      reads=[k_("ps", pz)], writes=[k_("abT", bidx)])
                            elif kind == "g":
                                S.add("act", lambda e, osl=osl, tl=tl, n=n, pz=pz: e.activation(
                                    out=orow[osl][:, tl:tl + n], in_=ps[pz][:, :n], func=AF.Silu),
                                    reads=[k_("ps", pz)], writes=[k_("orow", osl, bidx)])
                            elif kind == "u":
                                S.add("act", lambda e, tl=tl, n=n, pz=pz, h=h: e.activation(
                                    out=urow[h % 2][:, tl:tl + n], in_=ps[pz][:, :n], func=AF.Gelu_apprx_tanh),
                                    reads=[k_("ps", pz)], writes=[k_("urow", h % 2, bidx)])
                            elif kind == "vm":
                                S.add("act", lambda e, cs=cs, tl=tl, n=n, pz=pz: e.activation(
                                    out=crow[cs][:, tl:tl + n], in_=ps[pz][:, :n], func=AF.Gelu_apprx_tanh),
                                    reads=[k_("ps", pz)], writes=[k_("crow", cs, bidx)])
                            yield
                        if conv:
                            for b, (t0, n, _) in enumerate(blks):
                                pc = 2 + nxt("c", 2)

                                def cv(e, zs=zs, ds=ds, t0=t0, n=n, pc=pc):
                                    ins = None
                                    base = zcol(t0) - 2
                                    for tap in range(5):
                                        ins = e.matmul(ps[pc][:, :n], lhsT=dgj[ds][:, tap, :],
                                                       rhs=zrow[zs][:, base + tap:base + tap + n],
                                                       start=(tap == 0), stop=(tap == 4))
                                    return ins
                                S.add("pe", cv, reads=[k_("dgj", ds)] + [k_("zrow", zs, bb) for bb in (b - 1, b, b + 1) if 0 <= bb < 5],
                                      writes=[k_("ps", pc)])
                                if kind == "v":
                                    S.add("act", lambda e, osl=osl, t0=t0, n=n, pc=pc: e.activation(
                                        out=orow[osl][:, t0:t0 + n], in_=ps[pc][:, :n], func=AF.Silu),
                                        reads=[k_("ps", pc)], writes=[k_("orow", osl, b)])
                                else:
                                    S.add("act", lambda e, cs=cs, t0=t0, n=n, pc=pc: e.activation(
                                        out=crow[cs][:, t0:t0 + n], in_=ps[pc][:, :n], func=AF.Silu),
                                        reads=[k_("ps", pc)], writes=[k_("crow", cs, b)])
                                yield
                        if kind in ("q", "k", "vm"):
                            for b, (t0, n, _) in enumerate(blks):
                                bidx = b if len(blks) == 5 else b + 1
                                tl = t0 if kind != "vm" else t0 - T_CTX
                                si = nxt("sq", 2)
                                pst = 4 + nxt("st", 2)
                                S.add("pool", lambda e, si=si, cs=cs, tl=tl, n=n: e.tensor_tensor(
                                    out=sqb[si][:, :n], in0=crow[cs][:, tl:tl + n], in1=crow[cs][:, tl:tl + n], op=ALU.mult),
                                    reads=[k_("crow", cs, bidx)], writes=[k_("sqb", si)])
                                S.add("pe", lambda e, si=si, n=n, pst=pst: e.matmul(ps[pst][:, :n], lhsT=ones_b[:], rhs=sqb[si][:, :n],
                                                                                    start=True, stop=True),
                                      reads=[k_("sqb", si), k_("ident")], writes=[k_("ps", pst)])
                                S.add("act", lambda e, tl=tl, n=n, pst=pst, kind=kind, cs=cs: e.activation(
                                    out=lnrow[cs][:, tl:tl + n], in_=ps[pst][:, :n], func=AF.Ln,
                                    scale=(1.0 / P if kind == "vm" else 1.0), bias=EPS),
                                    reads=[k_("ps", pst)], writes=[k_("lnrow", cs, bidx)])
                                yield
                            for b, (t0, n, _) in enumerate(blks):
                                bidx = b if len(blks) == 5 else b + 1
                                tl = t0 if kind != "vm" else t0 - T_CTX
                                ri = nxt("ri", 2)
                                qb = float(np.log(float(P) ** -0.5)) if kind == "q" else 0.0
                                S.add("act", lambda e, ri=ri, tl=tl, n=n, qb=qb, cs=cs: e.activation(
                                    out=rinv[ri][:, :n], in_=lnrow[cs][:, tl:tl + n], func=AF.Exp, scale=-0.5, bias=qb),
                                    reads=[k_("lnrow", cs, bidx)], writes=[k_("rinv", ri)])
                                if kind == "vm":
                                    S.add("dve", lambda e, ri=ri, cs=cs, osl=osl, tl=tl, n=n, h=h: e.scalar_tensor_tensor(
                                        out=orow[osl][:, tl:tl + n], in0=crow[cs][:, tl:tl + n], scalar=mg[:, h:h + 1],
                                        in1=rinv[ri][:, :n], op0=ALU.mult, op1=ALU.mult),
                                        reads=[k_("crow", cs, bidx), k_("rinv", ri), k_("pc", 6)], writes=[k_("orow", osl, bidx)])
                                else:
                                    S.add("dve", lambda e, ri=ri, cs=cs, osl=osl, tl=tl, n=n: e.tensor_tensor(
                                        out=orow[osl][:, tl:tl + n], in0=crow[cs][:, tl:tl + n], in1=rinv[ri][:, :n], op=ALU.mult),
                                        reads=[k_("crow", cs, bidx), k_("rinv", ri)], writes=[k_("orow", osl, bidx)])
                                yield
                        if kind in ("q", "k"):
                            dstT = (QTs if kind == "q" else KTs)[h * P:(h + 1) * P, :]
                            S.add("sp", lambda e, osl=osl, dstT=dstT: e.dma_start(out=dstT, in_=orow[osl][:]), dma=True,
                                  reads=[k_("orow", osl, b) for b in range(5)])
                        if kind == "g":
                            S.add("sp", lambda e, osl=osl, h=h: e.dma_start(out=SGT[h * P:(h + 1) * P, :], in_=orow[osl][:, 0:T_LAT]),
                                  dma=True, reads=[k_("orow", osl, b) for b in range(1, 5)])
                        if kind in ("k", "v"):
                            tsl = nxt("ts", 2)
                            for g8 in range(3):
                                cnt8 = 8 if g8 < 2 else 2
                                pt = nxt("tr", 2)

                                def trp(e, osl=osl, g8=g8, cnt8=cnt8, pt=pt):
                                    ins = None
                                    for j in range(cnt8):
                                        tt = g8 * 8 + j
                                        ins = e.transpose(out=psb[pt][:, j * P:(j + 1) * P], in_=orow[osl][:, tt * P:(tt + 1) * P],
                                                          identity=ident_b[:])
                                    return ins
                                S.add("pe", trp, reads=[k_("orow", osl, b) for b in range(5)] + [k_("ident")],
                                      writes=[k_("ps", 6 + pt)])
                                S.add("act", lambda e, tsl=tsl, g8=g8, cnt8=cnt8, pt=pt: e.activation(
                                    out=tst[tsl][:, g8 * 8:g8 * 8 + cnt8, :],
                                    in_=psb[pt][:, 0:cnt8 * P].rearrange("p (a j) -> p a j", a=cnt8), func=AF.Copy),
                                    reads=[k_("ps", 6 + pt)], writes=[k_("tst", tsl, g8)])
                                yield
                            dtok = (KTOK if kind == "k" else VTOK).rearrange("(tt p) c -> p tt c", p=P)[:, :, h * P:(h + 1) * P]
                            S.add("sp", lambda e, tsl=tsl, dtok=dtok: e.dma_start(out=dtok, in_=tst[tsl][:]), dma=True,
                                  reads=[k_("tst", tsl, g8) for g8 in range(3)])
                        if kind == "ab":
                            for g16 in range(2):
                                c16 = 16 if g16 == 0 else 2
                                pb = 2 + g16

                                def trab(e, g16=g16, c16=c16, pb=pb):
                                    ins = None
                                    for j in range(c16):
                                        tt = g16 * 16 + j
                                        ins = e.transpose(out=ps[pb][:, j * 32:(j + 1) * 32], in_=abT[:, tt * P:(tt + 1) * P],
                                                          identity=ident_f[:32, :32])
                                    return ins
                                S.add("pe", trab, reads=[k_("abT", b) for b in range(5)] + [k_("ident")], writes=[k_("ps", pb)])
                                S.add("dve", lambda e, g16=g16, c16=c16, pb=pb: e.tensor_copy(
                                    out=ab_tok[:, g16 * 16:g16 * 16 + c16, :],
                                    in_=ps[pb][:, 0:c16 * 32].rearrange("p (a j) -> p a j", a=c16)),
                                    reads=[k_("ps", pb)], writes=[k_("ab_tok", g16)])
                                yield
                        if kind == "vm":
                            obs = nxt("ob", 2)
                            for b in range(4):
                                pt = nxt("tr", 2)
                                vs = nxt("vm", 2)

                                def trv(e, osl=osl, b=b, pt=pt):
                                    ins = None
                                    for c4 in range(4):
                                        col = b * 512 + c4 * P
                                        ins = e.transpose(out=psb[pt][:, c4 * P:(c4 + 1) * P], in_=orow[osl][:, col:col + P],
                                                          identity=ident_b[:])
                                    return ins
                                S.add("pe", trv, reads=[k_("orow", osl, b + 1), k_("ident")], writes=[k_("ps", 6 + pt)])
                                S.add("act", lambda e, vs=vs, pt=pt: e.activation(
                                    out=vmt[vs][:], in_=psb[pt][:, 0:512].rearrange("p (a j) -> p a j", a=4), func=AF.Copy),
                                    reads=[k_("ps", 6 + pt)], writes=[k_("vmt", vs)])
                                pm = 2 + nxt("c", 2)

                                def smm(e, vs=vs, pm=pm, h=h):
                                    ins = None
                                    for c4 in range(4):
                                        ins = e.matmul(ps[pm][:, c4 * P:(c4 + 1) * P], lhsT=vmt[vs][:, c4, :], rhs=WsT[:, h, :],
                                                       start=True, stop=True)
                                    return ins
                                S.add("pe", smm, reads=[k_("vmt", vs), k_("WsT")], writes=[k_("ps", pm)])
                                ti = nxt("t1", 2)
                                S.add("dve", lambda e, ti=ti, pm=pm, h=h: e.tensor_tensor(
                                    out=t1[ti][:].rearrange("p (a j) -> p a j", a=4),
                                    in0=ps[pm][:].rearrange("p (a j) -> p a j", a=4),
                                    in1=sb_bc[:, h * P:(h + 1) * P].unsqueeze(1).to_broadcast([P, 4, P]), op=ALU.add),
                                    reads=[k_("ps", pm), k_("sb_bc")], writes=[k_("t1", ti)])
                                S.add("pool", lambda e, ti=ti, obs=obs, b=b, h=h: e.tensor_tensor(
                                    out=obrow[obs][:, b * 512:(b + 1) * 512], in0=t1[ti][:], in1=urow[h % 2][:, b * 512:(b + 1) * 512],
                                    op=ALU.mult),
                                    reads=[k_("t1", ti), k_("urow", h % 2, b + 1)], writes=[k_("obrow", obs, b)])
                                yield
                            S.add("sp", lambda e, obs=obs, h=h: e.dma_start(out=OB[h * P:(h + 1) * P, :], in_=obrow[obs][:]),
                                  dma=True, reads=[k_("obrow", obs, b) for b in range(4)])

                    pending = list(jobs)
                    active = []
                    while pending or active:
                        while pending and len(active) < 2:
                            active.append(job_gen(*pending.pop(0)))
                        for g_ in list(active):
                            try:
                                next(g_)
                            except StopIteration:
                                active.remove(g_)

                    S.end_phase()

        def dn_phase(oacc):
            QTv = QTs.rearrange("(h p) t -> p h t", p=P)
            KTv = KTs.rearrange("(h p) t -> p h t", p=P)
            H8 = [P, NH, P]
            NHC = 4
            NCH = NH // NHC
            HC = [P, NHC, P]
            with ExitStack() as pes:
                def T_(name, shape, dt):
                    return pes.enter_context(nc.sbuf_tensor(name, shape, dt))
                mkf = T_("mkf", [P, 5, P], F32)
                mkb = T_("mkb", [P, 16, P], BF16)
                ones_f = T_("ones_f", [P, P], F32)
                al_bc = T_("al_bc", [P, 16], F32)
                dt_bc = T_("dt_bc", [P, 16], F32)
                pre = {nm: T_("pre_" + nm, [P, NTT, 16], F32) for nm in
                       ("xa", "t0", "t1", "g", "beta", "l2", "gc", "ngc", "gcl", "eg", "kbg", "kdec", "gtb", "gl0", "gl1")}
                ea = T_("pre_ea", [P, 16], F32)
                qTt = [[T_(f"qTt{d}{i}", H8, BF16) for i in range(2)] for d in range(2)]
                kTt = [[T_(f"kTt{d}{i}", H8, BF16) for i in range(2)] for d in range(2)]
                ktk = [[T_(f"ktk{d}{i}", H8, BF16) for i in range(2)] for d in range(2)]
                vtk = [[T_(f"vtk{d}{i}", H8, BF16) for i in range(2)] for d in range(2)]
                dg = [T_(f"dg{d}", H8, F32) for d in range(2)]
                EL = [T_(f"EL{d}", H8, BF16) for d in range(2)]
                ET = [T_(f"ET{d}", H8, BF16) for d in range(2)]
                EG = [T_(f"EG{d}", H8, BF16) for d in range(2)]
                Lm = [T_(f"Lm{d}", H8, BF16) for d in range(2)]
                qkT = [T_(f"qkT{d}", H8, BF16) for d in range(2)]
                Cb = [[T_(f"Cb{d}{i}", H8, BF16) for i in range(2)] for d in range(2)]
                Xb = [[T_(f"Xb{d}{i}", H8, BF16) for i in range(2)] for d in range(2)]
                Ub = [[T_(f"Ub{d}{i}", H8, BF16) for i in range(2)] for d in range(2)]
                Vb = [T_(f"Vb{d}", H8, BF16) for d in range(2)]
                vb = [T_(f"vb{d}", H8, BF16) for d in range(2)]
                kbg = [T_(f"kbg{d}", H8, BF16) for d in range(2)]
                kdc = [T_(f"kdc{d}", H8, BF16) for d in range(2)]
                qd = [T_(f"qd{d}", H8, BF16) for d in range(2)]
                u_sb = [T_(f"u_sb{d}", H8, F32) for d in range(2)]
                wT_sb = [T_(f"wT_sb{d}", H8, BF16) for d in range(2)]
                vn = [T_(f"vn{d}", H8, BF16) for d in range(2)]
                S32 = [T_(f"S32{d}", H8, F32) for d in range(2)]
                Sb = [T_(f"Sb{d}", H8, BF16) for d in range(2)]

                S.add("sp", lambda e: e.dma_start(out=mkf[:], in_=mkf_d.rearrange("m p f -> p m f")), dma=True, writes=[k_("mkf")])
                S.add("pool", lambda e: e.dma_start(out=mkb[:], in_=mkb_d.rearrange("m p f -> p m f")), dma=True, writes=[k_("mkb")])
                S.add("sp", lambda e: e.dma_start(out=al_bc[:], in_=alog_d.partition_broadcast(P)), dma=True, writes=[k_("al")])
                S.add("sp", lambda e: e.dma_start(out=dt_bc[:], in_=dtb_d.partition_broadcast(P)), dma=True, writes=[k_("dtb")])

                def init(e):
                    e.memset(ones_f[:], 1.0)
                    for d in range(2):
                        e.memset(S32[d][:], 0.0)
                        e.memset(Sb[d][:], 0.0)
                    return e.memset(oacc[:], 0.0)
                S.add("pool", init, writes=[k_("ones_f")] + [k_(nm, d, hh) for nm in ("S32", "Sb") for d in range(2) for hh in range(NCH)] +
                      [k_("oacc", t, hh) for t in range(2, NTT) for hh in range(NCH)])

                a_ap = ab_tok[:, :, 0:16]
                b_ap = ab_tok[:, :, 16:32]
                bc18 = lambda t: t[:].unsqueeze(1).to_broadcast([P, NTT, 16])
                pk = lambda *n: [k_("pre", x) for x in n]
                S.add("dve", lambda e: e.tensor_tensor(out=pre["xa"][:], in0=a_ap, in1=bc18(dt_bc), op=ALU.add),
                      reads=[k_("dtb")], writes=pk("xa"))
                S.add("act", lambda e: e.activation(out=pre["t0"][:], in_=pre["xa"][:], func=AF.Abs),
                      reads=pk("xa"), writes=pk("t0"))
                S.add("act", lambda e: e.activation(out=pre["t0"][:], in_=pre["t0"][:], func=AF.Exp, scale=-1.0),
                      reads=pk("t0"), writes=pk("t0"))
                S.add("act", lambda e: e.activation(out=pre["t0"][:], in_=pre["t0"][:], func=AF.Ln, bias=1.0),
                      reads=pk("t0"), writes=pk("t0"))
                S.add("dve", lambda e: e.scalar_tensor_tensor(out=pre["t1"][:], in0=pre["xa"][:], scalar=0.0, in1=pre["t0"][:],
                                                              op0=ALU.max, op1=ALU.add),
                      reads=pk("xa", "t0"), writes=pk("t1"))
                S.add("act", lambda e: e.activation(out=ea[:], in_=al_bc[:], func=AF.Exp), reads=[k_("al")], writes=pk("ea"))
                S.add("dve", lambda e: e.scalar_tensor_tensor(out=pre["g"][:], in0=pre["t1"][:], scalar=-1.0, in1=bc18(ea),
                                                              op0=ALU.mult, op1=ALU.mult),
                      reads=pk("t1", "ea"), writes=pk("g"))
                S.add("act", lambda e: e.activation(out=pre["beta"][:], in_=b_ap, func=AF.Exp, scale=-1.0), writes=pk("beta"))
                S.add("dve", lambda e: e.tensor_scalar(out=pre["beta"][:], in0=pre["beta"][:], scalar1=1.0, scalar2=None, op0=ALU.add),
                      reads=pk("beta"), writes=pk("beta"))
                S.add("act", lambda e: e.activation(out=pre["l2"][:], in_=pre["beta"][:], func=AF.Ln), reads=pk("beta"), writes=pk("l2"))
                S.add("dve", lambda e: e.reciprocal(out=pre["beta"][:], in_=pre["beta"][:]), reads=pk("beta", "l2"), writes=pk("beta"))

                def cums(e):
                    for d in range(2):
                        e.matmul(ps[0][:, d * 144:(d + 1) * 144], lhsT=mkf[:, d, :], rhs=pre["g"][:, :, d * 8:(d + 1) * 8],
                                 start=True, stop=True)
                    return e.matmul(ps[1][:, 0:288], lhsT=mkf[:, 2, :], rhs=pre["g"][:], start=True, stop=True)
                S.add("pe", cums, reads=pk("g") + [k_("mkf")], writes=[k_("ps", 0), k_("ps", 1)])

                def cums_ev(e):
                    for d in range(2):
                        e.tensor_copy(out=pre["gc"][:, :, d * 8:(d + 1) * 8],
                                      in_=ps[0][:, d * 144:(d + 1) * 144].rearrange("p (t c) -> p t c", c=8))
                    return e.tensor_copy(out=pre["gtb"][:], in_=ps[1][:, 0:288].rearrange("p (t c) -> p t c", c=16))
                S.add("dve", cums_ev, reads=[k_("ps", 0), k_("ps", 1)], writes=pk("gc", "gtb"))

                def sels(e):
                    e.matmul(ps[2][:, 0:288], lhsT=mkf[:, 3, :], rhs=pre["g"][:], start=True, stop=True)
                    return e.matmul(ps[3][:, 0:288], lhsT=mkf[:, 4, :], rhs=pre["g"][:], start=True, stop=True)
                S.add("pe", sels, reads=pk("g") + [k_("mkf")], writes=[k_("ps", 2), k_("ps", 3)])

                def sels_ev(e):
                    e.activation(out=pre["gl0"][:], in_=ps[2][:, 0:288].rearrange("p (t c) -> p t c", c=16), func=AF.Exp)
                    return e.activation(out=pre["gl1"][:], in_=ps[3][:, 0:288].rearrange("p (t c) -> p t c", c=16), func=AF.Exp)
                S.add("act", sels_ev, reads=[k_("ps", 2), k_("ps", 3)], writes=pk("gl0", "gl1"))
                S.add("dve", lambda e: e.tensor_scalar(out=pre["ngc"][:], in0=pre["gc"][:], scalar1=-1.0, scalar2=None, op0=ALU.mult),
                      reads=pk("gc"), writes=pk("ngc"))
                S.add("dve", lambda e: e.tensor_tensor(out=pre["gcl"][:], in0=pre["gc"][:], in1=pre["l2"][:], op=ALU.subtract),
                      reads=pk("gc", "l2"), writes=pk("gcl"))
                S.add("act", lambda e: e.activation(out=pre["eg"][:], in_=pre["gc"][:], func=AF.Exp), reads=pk("gc"), writes=pk("eg"))
                S.add("dve", lambda e: e.tensor_tensor(out=pre["kbg"][:], in0=pre["eg"][:], in1=pre["beta"][:], op=ALU.mult),
                      reads=pk("eg", "beta"), writes=pk("kbg"))
                S.add("dve", lambda e: e.tensor_tensor(out=pre["kdec"][:], in0=pre["gtb"][:], in1=pre["gc"][:], op=ALU.subtract),
                      reads=pk("gtb", "gc"), writes=pk("kdec"))
                S.add("act", lambda e: e.activation(out=pre["kdec"][:], in_=pre["kdec"][:], func=AF.Exp), reads=pk("kdec"), writes=pk("kdec"))
                PRE_ALL = pk("gc", "ngc", "gcl", "eg", "kbg", "kdec", "beta", "gl0", "gl1")

                orders = {0: list(range(NTT)), 1: [1, 0] + list(range(NTT - 1, 1, -1))}

                def emit_loads(d, step):
                    tile = orders[d][step]
                    tc0 = tile * P
                    sl = step % 2
                    S.add("sp", lambda e: e.dma_start(out=qTt[d][sl][:], in_=QTv[:, :, tc0:tc0 + P]), dma=True, writes=[k_("qTt", d, sl)])
                    S.add("sp", lambda e: e.dma_start(out=kTt[d][sl][:], in_=KTv[:, :, tc0:tc0 + P]), dma=True, writes=[k_("kTt", d, sl)])
                    S.add("sp", lambda e: e.dma_start(out=ktk[d][sl][:].rearrange("p h j -> p (h j)"), in_=KTOK[tc0:tc0 + P, :]),
                          dma=True, writes=[k_("ktk", d, sl)])
                    S.add("sp", lambda e: e.dma_start(out=vtk[d][sl][:].rearrange("p h j -> p (h j)"), in_=VTOK[tc0:tc0 + P, :]),
                          dma=True, writes=[k_("vtk", d, sl)])

                def chain(d, hh):
                    hs = slice(hh * NHC, hh * NHC + NHC)
                    cbank = (d * NCH + hh) * 2
                    bi = [0]

                    def nb():
                        v = (cbank + (bi[0] % 2), 0)
                        bi[0] += 1
                        return v
                    K = lambda nm, *x: k_(nm, d, hh, *x)
                    identrep = ident_b[:].unsqueeze(1).to_broadcast(HC)
                    identrep_f = ident_f[:].unsqueeze(1).to_broadcast(HC)

                    def mrep(i):
                        return mkb[:, i, :].unsqueeze(1).to_broadcast(HC)

                    def bcf(nm, tile):
                        return pre[nm][:, tile, d * 8 + hh * NHC:d * 8 + hh * NHC + NHC].unsqueeze(2).to_broadcast(HC)

                    def pv(b):
                        return ps[b[0]][:, b[1]:b[1] + NHC * P].rearrange("p (h j) -> p h j", h=NHC)

                    def mm4(b, lhs_fn, rhs_fn):
                        def f(e):
                            ins = None
                            for j in range(NHC):
                                h = hh * NHC + j
                                ins = e.matmul(ps[b[0]][:, b[1] + j * P:b[1] + (j + 1) * P], lhsT=lhs_fn(h), rhs=rhs_fn(h), start=True, stop=True)
                            return ins
                        return f

                    def do_step(step):
                        tile = orders[d][step]
                        sl = step % 2
                        if hh == 0:
                            if step == 0:
                                emit_loads(d, 0)
                            if step + 1 < NTT:
                                emit_loads(d, step + 1)
                        q_, k_t, kk_, vv_ = qTt[d][sl], kTt[d][sl], ktk[d][sl], vtk[d][sl]
                        kq, kk, kkt, kv = k_("qTt", d, sl), k_("kTt", d, sl), k_("ktk", d, sl), k_("vtk", d, sl)
                        S.add("pool", lambda e, tile=tile: e.tensor_tensor(out=dg[d][:, hs, :], in0=identrep_f, in1=bcf("gc", tile), op=ALU.mult),
                              reads=PRE_ALL + [k_("ident")], writes=[K("dg")])
                        yield
                        b1 = nb()
                        S.add("pe", mm4(b1, lambda h: ones_f[:], lambda h: dg[d][:, h, :]),
                              reads=[K("dg"), k_("ones_f")], writes=[k_("ps", b1[0])])
                        yield

                        def exps(e, tile=tile, b1=b1):
                            for j in range(NHC):
                                h = hh * NHC + j
                                c = d * 8 + h
                                src_ = ps[b1[0]][:, b1[1] + j * P:b1[1] + (j + 1) * P]
                                e.activation(out=EL[d][:, h, :], in_=src_, func=AF.Exp,
                                             scale=-1.0, bias=pre["gcl"][:, tile, c:c + 1])
                                e.activation(out=ET[d][:, h, :], in_=src_, func=AF.Exp,
                                             scale=1.0, bias=pre["ngc"][:, tile, c:c + 1])
                            return e.activation(out=EG[d][:, hs, :], in_=pv(b1), func=AF.Exp)
                        S.add("act", exps, reads=[k_("ps", b1[0])] + PRE_ALL, writes=[K("EL"), K("ET"), K("EG")])
                        yield
                        b2 = nb()
                        S.add("pe", mm4(b2, lambda h: k_t[:, h, :], lambda h: k_t[:, h, :]), reads=[kk], writes=[k_("ps", b2[0])])
                        yield
                        S.add("dve", lambda e: e.scalar_tensor_tensor(out=EL[d][:, hs, :], in0=EL[d][:, hs, :], scalar=1.0, in1=mrep(0 + d),
                                                                      op0=ALU.min, op1=ALU.mult),
                              reads=[K("EL"), k_("mkb")], writes=[K("EL")])
                        yield
                        S.add("dve", lambda e, b2=b2: e.tensor_tensor(out=Lm[d][:, hs, :], in0=pv(b2), in1=EL[d][:, hs, :], op=ALU.mult),
                              reads=[K("EL"), k_("ps", b2[0])], writes=[K("Lm")])
                        yield

                        def mkC(l, slot):
                            S.add("dve" if l in (3, 5) else "pool", lambda e: e.tensor_tensor(out=Cb[d][slot][:, hs, :], in0=Lm[d][:, hs, :], in1=mrep(4 + 6 * d + l), op=ALU.mult),
                                  reads=[K("Lm"), k_("mkb")], writes=[K("Cb", slot)])
                        mkC(0, 0)
                        yield
                        b4 = nb()
                        S.add("pe", mm4(b4, lambda h: Cb[d][0][:, h, :], lambda h: ident_b[:]),
                              reads=[K("Cb", 0), k_("ident")], writes=[k_("ps", b4[0])])
                        S.add("dve", lambda e: e.tensor_tensor(out=Xb[d][0][:, hs, :], in0=identrep, in1=Cb[d][0][:, hs, :], op=ALU.subtract),
                              reads=[K("Cb", 0), k_("ident")], writes=[K("Xb", 0)])
                        mkC(1, 1)
                        yield
                        b3 = nb()
                        S.add("pe", mm4(b3, lambda h: k_t[:, h, :], lambda h: q_[:, h, :]), reads=[kk, kq], writes=[k_("ps", b3[0])])
                        S.add("dve", lambda e: e.scalar_tensor_tensor(out=ET[d][:, hs, :], in0=ET[d][:, hs, :], scalar=1.0, in1=mrep(2 + d),
                                                                      op0=ALU.min, op1=ALU.mult),
                              reads=[K("ET"), k_("mkb")], writes=[K("ET")])
                        yield
                        S.add("dve", lambda e, b3=b3: e.tensor_tensor(out=qkT[d][:, hs, :], in0=pv(b3), in1=ET[d][:, hs, :], op=ALU.mult),
                              reads=[K("ET"), k_("ps", b3[0])], writes=[K("qkT")])
                        S.add("dve", lambda e, b4=b4: e.tensor_tensor(out=Ub[d][0][:, hs, :], in0=identrep, in1=pv(b4), op=ALU.subtract),
                              reads=[k_("ps", b4[0]), k_("ident")], writes=[K("Ub", 0)])
                        yield
                        cur = 0
                        for l in range(1, 6):
                            cslot = l % 2
                            if l > 1:
                                mkC(l, cslot)
                                yield
                            ba = nb()
                            S.add("pe", mm4(ba, lambda h, cslot=cslot: Cb[d][cslot][:, h, :], lambda h, cur=cur: Ub[d][cur][:, h, :]),
                                  reads=[K("Cb", cslot), K("Ub", cur)], writes=[k_("ps", ba[0])])
                            yield
                            S.add("dve", lambda e, ba=ba: e.tensor_tensor(out=Vb[d][:, hs, :], in0=identrep, in1=pv(ba), op=ALU.subtract),
                                  reads=[k_("ps", ba[0]), k_("ident")], writes=[K("Vb")])
                            yield
                            if l < 5:
                                bb = nb()
                                S.add("pe", mm4(bb, lambda h: Vb[d][:, h, :], lambda h, cur=cur: Xb[d][cur][:, h, :]),
                                      reads=[K("Vb"), K("Xb", cur)], writes=[k_("ps", bb[0])])
                            bc_ = nb()
                            S.add("pe", mm4(bc_, lambda h, cur=cur: Xb[d][cur][:, h, :], lambda h: Vb[d][:, h, :]),
                                  reads=[K("Vb"), K("Xb", cur)], writes=[k_("ps", bc_[0])])
                            yield
                            if l < 5:
                                S.add("act", lambda e, cur=cur, bb=bb: e.activation(out=Xb[d][1 - cur][:, hs, :], in_=pv(bb), func=AF.Copy),
                                      reads=[k_("ps", bb[0])], writes=[K("Xb", 1 - cur)])
                            S.add("act", lambda e, cur=cur, bc_=bc_: e.activation(out=Ub[d][1 - cur][:, hs, :], in_=pv(bc_), func=AF.Copy),
                                  reads=[k_("ps", bc_[0])], writes=[K("Ub", 1 - cur)])
                            yield
                            cur = 1 - cur
                        Uf = Ub[d][cur]
                        UK = K("Ub", cur)
                        S.add("pool", lambda e, tile=tile: e.tensor_tensor(out=vb[d][:, hs, :], in0=vv_[:, hs, :], in1=bcf("beta", tile), op=ALU.mult),
                              reads=[kv] + PRE_ALL, writes=[K("vb")])
                        S.add("pool", lambda e, tile=tile: e.tensor_tensor(out=kbg[d][:, hs, :], in0=kk_[:, hs, :], in1=bcf("kbg", tile), op=ALU.mult),
                              reads=[kkt] + PRE_ALL, writes=[K("kbg")])
                        yield
                        bu = nb()
                        S.add("pe", mm4(bu, lambda h: Uf[:, h, :], lambda h: vb[d][:, h, :]), reads=[UK, K("vb")], writes=[k_("ps", bu[0])])
                        bw_ = nb()
                        S.add("pe", mm4(bw_, lambda h: kbg[d][:, h, :], lambda h: Uf[:, h, :]), reads=[UK, K("kbg")], writes=[k_("ps", bw_[0])])
                        S.add("pool", lambda e, tile=tile: e.tensor_tensor(out=kdc[d][:, hs, :], in0=kk_[:, hs, :], in1=bcf("kdec", tile), op=ALU.mult),
                              reads=[kkt] + PRE_ALL, writes=[K("kdc")])
                        S.add("pool", lambda e: e.tensor_tensor(out=qd[d][:, hs, :], in0=q_[:, hs, :], in1=EG[d][:, hs, :], op=ALU.mult),
                              reads=[kq, K("EG")], writes=[K("qd")])
                        yield
                        S.add("act", lambda e, bu=bu: e.activation(out=u_sb[d][:, hs, :], in_=pv(bu), func=AF.Copy),
                              reads=[k_("ps", bu[0])], writes=[K("u_sb")])
                        S.add("act", lambda e, bw_=bw_: e.activation(out=wT_sb[d][:, hs, :], in_=pv(bw_), func=AF.Copy),
                              reads=[k_("ps", bw_[0])], writes=[K("wT_sb")])
                        yield
                        for ci in ((0, 1) if d == 0 else (1, 0)):
                            c0 = 64 * ci
                            bp = nb()

                            def wS(e, c0=c0, bp=bp):
                                ins = None
                                for j in range(NHC):
                                    h = hh * NHC + j
                                    ins = e.matmul(ps[bp[0]][c0:c0 + 64, bp[1] + j * P:bp[1] + (j + 1) * P], lhsT=wT_sb[d][:, h, c0:c0 + 64],
                                                   rhs=Sb[d][:, h, :], start=True, stop=True)
                                return ins
                            S.add("pe", wS, reads=[K("wT_sb"), K("Sb")], writes=[k_("ps", bp[0])])
                            yield
                            S.add("dve", lambda e, c0=c0, bp=bp: e.tensor_tensor(
                                out=vn[d][c0:c0 + 64, hs, :], in0=u_sb[d][c0:c0 + 64, hs, :],
                                in1=ps[bp[0]][c0:c0 + 64, bp[1]:bp[1] + NHC * P].rearrange("p (h j) -> p h j", h=NHC), op=ALU.subtract),
                                reads=[k_("ps", bp[0]), K("u_sb")], writes=[K("vn")])
                            yield
                            bs_ = nb()

                            def dS(e, c0=c0, bs_=bs_):
                                ins = None
                                for j in range(NHC):
                                    h = hh * NHC + j
                                    ins = e.matmul(ps[bs_[0]][:, bs_[1] + j * P:bs_[1] + (j + 1) * P], lhsT=kdc[d][c0:c0 + 64, h, :],
                                                   rhs=vn[d][c0:c0 + 64, h, :], start=True, stop=True)
                                return ins
                            if tile >= 2:
                                bo = nb()

                                def oT(e, c0=c0, bo=bo):
                                    ins = None
                                    for j in range(NHC):
                                        h = hh * NHC + j
                                        e.matmul(ps[bo[0]][:, bo[1] + j * 64:bo[1] + (j + 1) * 64], lhsT=Sb[d][:, h, :], rhs=qd[d][:, h, c0:c0 + 64],
                                                 start=True, stop=False)
                                        ins = e.matmul(ps[bo[0]][:, bo[1] + j * 64:bo[1] + (j + 1) * 64], lhsT=vn[d][c0:c0 + 64, h, :],
                                                       rhs=qkT[d][c0:c0 + 64, h, c0:c0 + 64], start=False, stop=True)
                                    return ins
                                S.add("pe", oT, reads=[K("Sb"), K("qd"), K("vn"), K("qkT")], writes=[k_("ps", bo[0])])
                            S.add("pe", dS, reads=[K("kdc"), K("vn")], writes=[k_("ps", bs_[0])])
                            yield

                            def supd(e, ci=ci, tile=tile, bs_=bs_):
                                ins = None
                                gl = pre["gl0"] if ci == 0 else pre["gl1"]
                                for j in range(NHC):
                                    h = hh * NHC + j
                                    c = d * 8 + h
                                    ins = e.scalar_tensor_tensor(out=S32[d][:, h, :], in0=S32[d][:, h, :], scalar=gl[:, tile, c:c + 1],
                                                                 in1=ps[bs_[0]][:, bs_[1] + j * P:bs_[1] + (j + 1) * P], op0=ALU.mult, op1=ALU.add)
                                return ins
                            S.add("dve", supd, reads=[k_("ps", bs_[0]), K("S32")] + PRE_ALL, writes=[K("S32")])
                            yield
                            S.add("act", lambda e: e.activation(out=Sb[d][:, hs, :], in_=S32[d][:, hs, :], func=AF.Copy),
                                  reads=[K("S32")], writes=[K("Sb")])
                            if tile >= 2:
                                oc = (tile - 2) * P + c0
                                S.add("dve", lambda e, bo=bo, oc=oc: e.tensor_tensor(
                                    out=oacc[:, hs, oc:oc + 64], in0=oacc[:, hs, oc:oc + 64],
                                    in1=ps[bo[0]][:, bo[1]:bo[1] + NHC * 64].rearrange("p (h j) -> p h j", h=NHC), op=ALU.add),
                                    reads=[k_("ps", bo[0]), k_("oacc", tile, hh)], writes=[k_("oacc", tile, hh)])
                            yield

                    for step in range(NTT):
                        yield from do_step(step)

                gens = [chain(d, hh) for d in range(2) for hh in range(NCH)]
                for g_ in gens:
                    next(g_)
                for g_, adv in zip(gens, (0, 24, 12, 36)):
                    for _ in range(adv):
                        next(g_)
                while gens:
                    for g_ in list(gens):
                        try:
                            next(g_)
                        except StopIteration:
                            gens.remove(g_)
                if DEBUG:
                    S.add("sp", lambda e: e.dma_start(out=OACC.rearrange("(h p) t -> p h t", p=P), in_=oacc[:]), dma=True,
                          reads=[k_("oacc", t, hh) for t in range(2, NTT) for hh in range(NCH)])
                S.end_phase()

        def g_phase(oacc):
            wout_v = wout_d.rearrange("(k p) n -> p k n", p=P)
            with ExitStack() as pes:
                oaT = pes.enter_context(nc.sbuf_tensor("oaT", [P, NH, T_LAT], BF16))
                obT = pes.enter_context(nc.sbuf_tensor("obT", [P, NH, T_LAT], BF16))
                woR = pes.enter_context(nc.sbuf_tensor("woR", [P, KT // 2, KT, 256], BF16))
                NS = 4
                sgt = [pes.enter_context(nc.sbuf_tensor(f"sgt{i}", [P, 512], BF16)) for i in range(NS)]
                sqg = [pes.enter_context(nc.sbuf_tensor(f"sqg{i}", [P, 512], BF16)) for i in range(NS)]
                lng = [pes.enter_context(nc.sbuf_tensor(f"lng{i}", [P, 512], F32)) for i in range(NS)]
                tg = [pes.enter_context(nc.sbuf_tensor(f"tg{i}", [P, 512], F32)) for i in range(NS)]
                x1 = [pes.enter_context(nc.sbuf_tensor(f"x1{i}", [P, 512], F32)) for i in range(3)]
                SBK = [0, 1, 4, 5]
                PYK = [2, 3, 6, 7]
                ctr = dict(n=0, x=0)
                S.add("sp", lambda e: e.dma_start(out=obT[:], in_=OB.rearrange("(h p) t -> p h t", p=P)), dma=True, writes=[k_("obT")])
                for dp in range(KT // 2):
                    S.add("pool", lambda e, dp=dp: e.dma_start(out=woR[:, dp, :, :], in_=wout_v[:, :, dp * 256:(dp + 1) * 256]),
                          dma=True, writes=[k_("wo", dp)])

                def norm(h, b):
                    sl = ctr["n"] % NS
                    pst = SBK[ctr["n"] % 4]
                    ctr["n"] += 1
                    cs_ = slice(b * 512, (b + 1) * 512)
                    S.add("sp", lambda e: e.dma_start(out=sgt[sl][:], in_=SGT[h * P:(h + 1) * P, cs_]),
                          dma=True, writes=[k_("sgt", sl)])
                    S.add("act", lambda e: e.activation(out=sqg[sl][:], in_=oacc[:, h, cs_], func=AF.Square),
                          writes=[k_("sqg", sl)])
                    S.add("pe", lambda e: e.matmul(ps[pst][:], lhsT=ones_b[:], rhs=sqg[sl][:], start=True, stop=True),
                          reads=[k_("sqg", sl), k_("ident")], writes=[k_("ps", pst)])
                    S.add("act", lambda e: e.activation(out=lng[sl][:], in_=ps[pst][:], func=AF.Ln, scale=1.0 / P, bias=EPS),
                          reads=[k_("ps", pst)], writes=[k_("lng", sl)])
                    S.add("act", lambda e: e.activation(out=lng[sl][:], in_=lng[sl][:], func=AF.Exp, scale=-0.5),
                          reads=[k_("lng", sl)], writes=[k_("lng", sl)])
                    S.add("dve", lambda e: e.scalar_tensor_tensor(
                        out=tg[sl][:], in0=oacc[:, h, cs_], scalar=hg[:, 0:1], in1=lng[sl][:], op0=ALU.mult, op1=ALU.mult),
                        reads=[k_("lng", sl), k_("hg")], writes=[k_("tg", sl)])
                    S.add("pool", lambda e: e.tensor_tensor(out=oaT[:, h, cs_], in0=tg[sl][:], in1=sgt[sl][:], op=ALU.mult),
                          reads=[k_("tg", sl), k_("sgt", sl)], writes=[k_("oaT", h, b)])

                def proj(b, dtile):
                    cs_ = slice(b * 512, (b + 1) * 512)
                    py = PYK[ctr["x"] % 4]
                    xs = ctr["x"] % 3
                    ctr["x"] += 1
                    dp, dj = dtile // 2, dtile % 2

                    def omm(e):
                        ins = None
                        for k in range(KT):
                            src = oaT if k < 8 else obT
                            ins = e.matmul(ps[py][:], lhsT=woR[:, dp, k, dj * P:(dj + 1) * P], rhs=src[:, k % 8, cs_],
                                           start=(k == 0), stop=(k == KT - 1))
                        return ins
                    S.add("pe", omm, reads=[k_("wo", dp), k_("obT")] + [k_("oaT", h, b) for h in range(NH)], writes=[k_("ps", py)])
                    tcs = slice(T_CTX + b * 512, T_CTX + (b + 1) * 512)
                    S.add("sp", lambda e: e.dma_start(out=x1[xs][:], in_=XT1v[:, dtile, tcs]), dma=True, writes=[k_("x1", xs)])
                    S.add("dve", lambda e: e.scalar_tensor_tensor(
                        out=x1[xs][:], in0=ps[py][:], scalar=gate[:, 1, dtile, 0:1], in1=x1[xs][:], op0=ALU.mult, op1=ALU.add),
                        reads=[k_("ps", py), k_("x1", xs), k_("mods")], writes=[k_("x1", xs)])
                    S.add("sp", lambda e: e.dma_start(out=XT2v[:, dtile, tcs], in_=x1[xs][:]), dma=True, reads=[k_("x1", xs)])

                for h in range(NH):
                    norm(h, 0)
                for b in range(4):
                    for h in range(NH):
                        if b + 1 < 4:
                            norm(h, b + 1)
                        proj(b, 2 * h)
                        proj(b, 2 * h + 1)
                S.end_phase()

        if STAGE >= 2:
            z_phase()
        if STAGE >= 3:
            with ExitStack() as oes:
                oacc = oes.enter_context(nc.sbuf_tensor("oacc", [P, NH, T_LAT], BF16))
                dn_phase(oacc)
                if STAGE >= 4:
                    g_phase(oacc)
        if STAGE >= 4:
            XFv = XT2v
        if STAGE >= 5:
            ffn_phase("f2", XT2v, XT3v, f2in_d, f2out_d, 2, blocks_lat)
            XFv = XT3v

        def out_phase(XSv, do_norm):
            with ExitStack() as pes:
                xT = [pes.enter_context(nc.sbuf_tensor(f"oxT{i}", [P, KT, 512], F32)) for i in range(2)]
                orow = [pes.enter_context(nc.sbuf_tensor(f"orow{i}", [P, D], F32)) for i in range(2)]
                sq = [pes.enter_context(nc.sbuf_tensor(f"osq{i}", [P, 512], BF16)) for i in range(4)]
                lnv = [pes.enter_context(nc.sbuf_tensor(f"olnv{i}", [P, 512], F32)) for i in range(2)]
                rstd = [pes.enter_context(nc.sbuf_tensor(f"orstd{i}", [P, 512], F32)) for i in range(2)]
                ctr = dict(o=0, b=0)
                nbk = len(blocks_lat)
                n = 512

                def e_load(bi):
                    t0 = blocks_lat[bi][0]
                    sl = bi % 2
                    S.add("sp", lambda e: e.dma_start(out=xT[sl][:, :, :n], in_=XSv[:, :, t0:t0 + n]),
                          dma=True, writes=[k_("xT", sl, k) for k in range(KT)])

                def e_stat_k(bi, k):
                    sl = bi % 2
                    SB = bi % 2
                    S.add("act", lambda e: e.activation(out=sq[k % 4][:, :n], in_=xT[sl][:, k, :n], func=AF.Square),
                          reads=[k_("xT", sl, k)], writes=[k_("sq", k % 4)])
                    S.add("pe", lambda e: e.matmul(ps[SB][:, :n], lhsT=ones_b[:], rhs=sq[k % 4][:, :n],
                                                   start=(k == 0), stop=(k == KT - 1)),
                          reads=[k_("sq", k % 4), k_("ident")], writes=[k_("ps", SB)])
                    if k == KT - 1:
                        S.add("act", lambda e: e.activation(out=lnv[sl][:, :n], in_=ps[SB][:, :n], func=AF.Ln, scale=1.0 / D, bias=EPS),
                              reads=[k_("ps", SB)], writes=[k_("lnv", sl)])
                        S.add("act", lambda e: e.activation(out=rstd[sl][:, :n], in_=lnv[sl][:, :n], func=AF.Exp, scale=-0.5),
                              reads=[k_("lnv", sl)], writes=[k_("rstd", sl)])

                def e_scale_k(bi, k):
                    sl = bi % 2
                    S.add("dve", lambda e: e.scalar_tensor_tensor(
                        out=xT[sl][:, k, :n], in0=xT[sl][:, k, :n], scalar=fg[:, k:k + 1],
                        in1=rstd[sl][:, :n], op0=ALU.mult, op1=ALU.mult),
                        reads=[k_("xT", sl, k), k_("rstd", sl), k_("pc", 4)], writes=[k_("xT", sl, k)])

                def e_tr(bi, tt):
                    t0 = blocks_lat[bi][0]
                    sl = bi % 2
                    osl = ctr["o"] % 2
                    ctr["o"] += 1
                    for q4 in range(4):
                        bank = 2 + ctr["b"] % 6
                        ctr["b"] += 1

                        def tr(e, q4=q4, bank=bank):
                            ins = None
                            for j in range(4):
                                k = q4 * 4 + j
                                ins = e.transpose(out=ps[bank][:, j * P:(j + 1) * P],
                                                  in_=xT[sl][:, k, tt * P:(tt + 1) * P], identity=ident_f[:])
                            return ins
                        S.add("pe", tr, reads=[k_("xT", sl, q4 * 4 + j) for j in range(4)] + [k_("ident")], writes=[k_("ps", bank)])
                        dst = orow[osl][:, q4 * 512:(q4 + 1) * 512]
                        if q4 % 2 == 0:
                            S.add("dve", lambda e, dst=dst, bank=bank: e.tensor_copy(out=dst, in_=ps[bank][:]),
                                  reads=[k_("ps", bank)], writes=[k_("orow", osl, q4)])
                        else:
                            S.add("act", lambda e, dst=dst, bank=bank: e.activation(out=dst, in_=ps[bank][:], func=AF.Copy),
                                  reads=[k_("ps", bank)], writes=[k_("orow", osl, q4)])
                    row0 = t0 - T_CTX + tt * P
                    S.add("sp", lambda e: e.dma_start(out=out_d[row0:row0 + P, :], in_=orow[osl][:]),
                          dma=True, reads=[k_("orow", osl, q) for q in range(4)])

                e_load(0)
                if do_norm:
                    for k in range(KT):
                        e_stat_k(0, k)
                    for k in range(KT):
                        e_scale_k(0, k)
                for bi in range(nbk):
                    nxt_ = bi + 1 < nbk
                    if nxt_:
                        e_load(bi + 1)
                    for tt in range(4):
                        e_tr(bi, tt)
                        if nxt_ and do_norm:
                            if tt < 2:
                                for k in range(tt * 8, tt * 8 + 8):
                                    e_stat_k(bi + 1, k)
                            else:
                                for k in range((tt - 2) * 8, (tt - 2) * 8 + 8):
                                    e_scale_k(bi + 1, k)
                S.end_phase()

        out_phase(XFv, STAGE >= 99)
        S.add("sp", lambda e: e.nop(), )
        S.add("act", lambda e: e.nop(), )
        S.emit()
    return nc


_CACHE = {}
_DBG = {}


def _make_masks():
    p = np.arange(P)[:, None]
    f = np.arange(P)[None, :]
    same = (p // 64) == (f // 64)
    mLf = same & (f < p)
    mLb = same & (f > p)
    mQf = same & (p <= f)
    mQb = same & (p >= f)
    mCf, mCb = [], []
    for l in range(6):
        s_ = 1 << l
        blk = (p // (2 * s_)) == (f // (2 * s_))
        mCf.append(blk & ((p % (2 * s_)) >= s_) & ((f % (2 * s_)) < s_))
        mCb.append(blk & ((p % (2 * s_)) < s_) & ((f % (2 * s_)) >= s_))
    sel0 = (p < 64) & (f >= 0)
    sel1 = (p >= 64) & (f >= 0)
    mb = np.stack([mLf, mLb, mQf, mQb] + mCf + mCb, 0).astype(np.float32)
    mf = np.stack([mQf, mQb, same, sel0, sel1], 0).astype(np.float32)
    return np.ascontiguousarray(mf), np.ascontiguousarray(mb)


def kernel(**inputs):
    import os
    f = lambda a: np.ascontiguousarray(np.asarray(a, dtype=np.float32))
    if "nc" not in _CACHE:
        _CACHE["nc"] = build_program()
    nc = _CACHE["nc"]
    ncores = int(os.environ.get("KCORES", "8"))
    x = f(inputs["x"])
    ctx = f(inputs["ctx"])
    c = f(inputs["c"])
    c_ctx = f(inputs["c_ctx"])
    mf, mb = _make_masks()
    shared = {
        "w_mod": f(inputs["w_mod"][0]),
        "b_mod": f(inputs["b_mod"][0]).reshape(144, P),
        "norm_g": f(inputs["norm_g"][0]).reshape(48, P),
        "ffn1_w_in": f(inputs["ffn1_w_in"][0]),
        "ffn1_w_out": f(inputs["ffn1_w_out"][0]),
        "w_in": f(inputs["w_in"][0]),
        "conv_w": f(inputs["conv_w"][0]).reshape(120, P),
        "a_log": f(inputs["a_log"][0]).reshape(1, 16),
        "dt_bias": f(inputs["dt_bias"][0]).reshape(1, 16),
        "head_norm_g": f(inputs["head_norm_g"][0]).reshape(1, P),
        "spatial_w": f(inputs["spatial_w"][0]),
        "spatial_b": f(inputs["spatial_b"][0]).reshape(1, NH * P),
        "mlp_norm_g": f(inputs["mlp_norm_g"][0]).reshape(8, P),
        "w_out": f(inputs["w_out"][0]),
        "ffn2_w_in": f(inputs["ffn2_w_in"][0]),
        "ffn2_w_out": f(inputs["ffn2_w_out"][0]),
        "final_g": f(inputs["final_g"]).reshape(16, P),
        "masks_f": mf,
        "masks_b": mb,
    }
    in_maps = []
    for b in range(ncores):
        m = dict(shared)
        m["x"] = x[b]
        m["ctx"] = ctx[b]
        m["c2"] = np.ascontiguousarray(np.stack([c[b], c_ctx], 0).reshape(32, P))
        in_maps.append(m)
    res = run_bass_kernel_spmd(nc, in_maps, core_ids=list(range(ncores)))
    if DEBUG:
        _DBG["res"] = res.results
    out = np.stack([np.asarray(r["out"], dtype=np.float32) for r in res.results], 0)
    if ncores < 8:
        out = np.concatenate([out, np.zeros((8 - ncores,) + out.shape[1:], np.float32)], 0)
    return out
```

```python
import numpy as np
from contextlib import ExitStack
import concourse.bass as bass
import concourse.mybir as mybir
from concourse.bass_utils import run_bass_kernel_spmd

F32 = mybir.dt.float32
BF16 = mybir.dt.bfloat16
AF = mybir.ActivationFunctionType
ALU = mybir.AluOpType

P = 128
D = 2048
KT = 16
T_LAT = 2048
T_CTX = 256
NTOK = T_LAT + T_CTX
NTT = NTOK // P
DFF = 5504
FT = DFF // P
NMOD = 9
EPS = 1e-6
IN_COLS = 6176
NH = 8
ENGS = ("pe", "act", "dve", "pool", "sp")
STAGE = 99
DEBUG = False
PAD_ROWS = 640


class Op:
    __slots__ = ("eng", "fn", "deps", "idx", "sig", "signo", "dma", "dsem", "dval", "prev_dval")

    def __init__(self, eng, fn, dma):
        self.eng = eng
        self.fn = fn
        self.dma = dma
        self.deps = set()
        self.sig = False
        self.signo = 0
        self.dsem = None
        self.dval = 0
        self.prev_dval = 0


class Sched:
    EP = 2048
    NDMA = {"sp": 20, "act": 6, "pool": 20}

    def __init__(self, nc, es):
        self.nc = nc
        self.es = es
        self.q = {e: [] for e in ENGS}
        self.lastw = {}
        self.readers = {}
        self.cnt = {e: 0 for e in ENGS}
        self.csem = {e: [] for e in ENGS}
        self.dsem = {e: [es.enter_context(nc.semaphore(f"d_{e}_{i}")) for i in range(n)]
                     for e, n in self.NDMA.items()}
        self.dcum = {e: [0] * n for e, n in self.NDMA.items()}
        self.dnext = {e: 0 for e in self.NDMA}
        self.seen = {}
        self.seen_d = {}
        self.bar = []
        self.bar_pending = set()
        self.last_op = {e: None for e in ENGS}
        self.phase_dmas = []

    def add(self, eng, fn, reads=(), writes=(), dma=False, deps=()):
        op = Op(eng, fn, dma)
        d = set(x for x in deps if x is not None)
        for k in reads:
            w = self.lastw.get(k)
            if w is not None:
                d.add(w)
        for k in writes:
            w = self.lastw.get(k)
            if w is not None:
                d.add(w)
            d.update(self.readers.get(k, ()))
        for k in reads:
            self.readers.setdefault(k, []).append(op)
        for k in writes:
            self.lastw[k] = op
            self.readers[k] = []
        if eng in self.bar_pending:
            d.update(self.bar)
            self.bar_pending.discard(eng)
        d.discard(op)
        op.deps = d
        for x in d:
            x.sig = True
        self.q[eng].append(op)
        self.last_op[eng] = op
        if dma:
            self.phase_dmas.append(op)
        return op

    def barrier(self):
        self.bar = [o for o in self.last_op.values() if o is not None] + list(self.phase_dmas)
        for o in self.bar:
            o.sig = True
        self.bar_pending = set(ENGS)
        self.lastw = {}
        self.readers = {}

    def _sem_for(self, eng, n):
        ep = (n - 1) // self.EP
        while len(self.csem[eng]) <= ep:
            self.csem[eng].append(self.es.enter_context(
                self.nc.semaphore(f"c_{eng}_{len(self.csem[eng])}")))
        return self.csem[eng][ep], (n - 1) % self.EP + 1

    def emit(self):
        nc = self.nc
        for e in ENGS:
            for op in self.q[e]:
                if op.dma:
                    i = self.dnext[e]
                    self.dnext[e] = (i + 1) % len(self.dsem[e])
                    op.dsem = (e, i)
                    op.prev_dval = self.dcum[e][i]
                    self.dcum[e][i] += 16
                    op.dval = self.dcum[e][i]
                elif op.sig:
                    self.cnt[e] += 1
                    op.signo = self.cnt[e]
        plans = {}
        for e in ENGS:
            plan = []
            for op in self.q[e]:
                need = {}
                dneed = {}
                for dd in op.deps:
                    if dd.dma:
                        dneed[dd.dsem] = max(dneed.get(dd.dsem, 0), dd.dval)
                    else:
                        if dd.eng == e and e == "pe":
                            continue
                        need[dd.eng] = max(need.get(dd.eng, 0), dd.signo)
                if op.dma and op.prev_dval > 0:
                    dneed[op.dsem] = max(dneed.get(op.dsem, 0), op.prev_dval)
                waits = []
                for pe_, n in need.items():
                    if n > self.seen.get((e, pe_), 0):
                        self.seen[(e, pe_)] = n
                        waits.append(self._sem_for(pe_, n))
                for ds, v in dneed.items():
                    if v > self.seen_d.get((e, ds), 0):
                        self.seen_d[(e, ds)] = v
                        waits.append((self.dsem[ds[0]][ds[1]], v))
                inc = None
                if op.dma:
                    inc = (self.dsem[op.dsem[0]][op.dsem[1]], 16)
                elif op.sig:
                    inc = (self._sem_for(e, op.signo)[0], 1)
                plan.append((waits, op.fn, inc))
            plans[e] = plan

        def run(engine, plan):
            for waits, fn, inc in plan:
                for s, v in waits:
                    engine.wait_ge(s, v)
                ins = fn(engine)
                if inc is not None:
                    ins.then_inc(inc[0], inc[1])

        with nc.Block() as block:
            @block.tensor
            def _(e):
                run(e, plans["pe"])

            @block.scalar
            def _(e):
                run(e, plans["act"])

            @block.vector
            def _(e):
                run(e, plans["dve"])

            @block.gpsimd
            def _(e):
                run(e, plans["pool"])

            @block.sync
            def _(e):
                run(e, plans["sp"])
        self.q = {e: [] for e in ENGS}

    def end_phase(self):
        self.barrier()
        self.emit()
        self.phase_dmas = []


def build_program():
    nc = bass.Bass("TRN2", target_bir_lowering=False)
    dt_in = {}

    def din(name, shape):
        t = nc.dram_tensor(name, list(shape), F32, kind="ExternalInput")
        dt_in[name] = t
        return t.ap()

    x_d = din("x", [T_LAT, D])
    ctx_d = din("ctx", [T_CTX, D])
    c2_d = din("c2", [32, P])
    wmod_d = din("w_mod", [D, NMOD * D])
    bmod_d = din("b_mod", [144, P])
    ng_d = din("norm_g", [48, P])
    f1in_d = din("ffn1_w_in", [D, 2 * DFF])
    f1out_d = din("ffn1_w_out", [DFF, D])
    win_d = din("w_in", [D, IN_COLS])
    convw_d = din("conv_w", [120, P])
    alog_d = din("a_log", [1, 16])
    dtb_d = din("dt_bias", [1, 16])
    hng_d = din("head_norm_g", [1, P])
    spw_d = din("spatial_w", [NH, P, P])
    spb_d = din("spatial_b", [1, NH * P])
    mng_d = din("mlp_norm_g", [8, P])
    wout_d = din("w_out", [D, D])
    f2in_d = din("ffn2_w_in", [D, 2 * DFF])
    f2out_d = din("ffn2_w_out", [DFF, D])
    fg_d = din("final_g", [16, P])
    mkf_d = din("masks_f", [5, P, P])
    mkb_d = din("masks_b", [16, P, P])
    out_d = nc.dram_tensor("out", [T_LAT, D], F32, kind="ExternalOutput").ap()

    skind = "ExternalOutput" if DEBUG else "Internal"
    PADT = nc.dram_tensor("padscr", [PAD_ROWS, D], F32, kind="Internal").ap()

    def scr(name, shape, dt):
        return nc.dram_tensor(name, list(shape), dt, kind=skind).ap()
    XT0 = scr("XT0", [D, NTOK], F32)
    XT1 = scr("XT1", [D, NTOK], F32)
    XT2 = scr("XT2", [D, NTOK], F32)
    XT3 = scr("XT3", [D, NTOK], F32)
    QTs = scr("QT", [NH * P, NTOK], BF16)
    KTs = scr("KT", [NH * P, NTOK], BF16)
    KTOK = scr("KTOK", [NTOK, NH * P], BF16)
    VTOK = scr("VTOK", [NTOK, NH * P], BF16)
    SGT = scr("SGT", [NH * P, T_LAT], BF16)
    OB = scr("OB", [NH * P, T_LAT], BF16)
    WBF = {}
    for tg_ in ("f1", "f2"):
        WBF[tg_] = (nc.dram_tensor(f"WA_{tg_}", [22, P, KT * 2 * 256], BF16, kind="Internal").ap(),
                    nc.dram_tensor(f"WO_{tg_}", [16, P, 22 * 256], BF16, kind="Internal").ap())
    OACC = scr("OACC", [NH * P, T_LAT], BF16) if DEBUG else None

    es = ExitStack()
    with es:
        S = Sched(nc, es)
        psg = [es.enter_context(nc.psum_tensor(f"psg{i}", [P, 1024], F32)) for i in range(4)]
        ps = [psg[i // 2][:, (i % 2) * 512:(i % 2 + 1) * 512] for i in range(8)]
        ident_f = es.enter_context(nc.sbuf_tensor("ident_f", [P, P], F32))
        ident_b = es.enter_context(nc.sbuf_tensor("ident_b", [P, P], BF16))
        ones_b = es.enter_context(nc.sbuf_tensor("ones_b", [P, P], BF16))
        sc = es.enter_context(nc.sbuf_tensor("sc", [P, KT, 2], BF16))
        bm = es.enter_context(nc.sbuf_tensor("bm", [P, 144], F32))
        ng = es.enter_context(nc.sbuf_tensor("ng", [P, 48], F32))
        fg = es.enter_context(nc.sbuf_tensor("fg", [P, 16], F32))
        modv = es.enter_context(nc.sbuf_tensor("modv", [P, 144, 2], F32))
        gs = es.enter_context(nc.sbuf_tensor("gs", [P, 3, KT, 2], F32))
        sh = es.enter_context(nc.sbuf_tensor("sh", [P, 3, KT, 2], F32))
        gate = es.enter_context(nc.sbuf_tensor("gate", [P, 3, KT, 2], F32))
        cw = es.enter_context(nc.sbuf_tensor("cw", [P, 120], F32))
        mg = es.enter_context(nc.sbuf_tensor("mg", [P, 8], F32))
        hg = es.enter_context(nc.sbuf_tensor("hg", [P, 1], F32))
        WsT = es.enter_context(nc.sbuf_tensor("WsT", [P, NH, P], BF16))
        sb_bc = es.enter_context(nc.sbuf_tensor("sb_bc", [P, NH * P], F32))
        ab_tok = es.enter_context(nc.sbuf_tensor("ab_tok", [P, NTT, 32], F32))

        def k_(name, *idx):
            return (name,) + idx

        S.add("pool", lambda e: e.memset(ident_f[:], 0.0), writes=[k_("ident_f")])
        S.add("pool", lambda e: e.affine_select(out=ident_f[:], in_=ident_f[:], pattern=[[-1, P]],
                                                compare_op=ALU.not_equal, fill=1.0, base=0, channel_multiplier=1),
              reads=[k_("ident_f")], writes=[k_("ident_f")])
        S.add("pool", lambda e: e.memset(ones_b[:], 1.0), writes=[k_("ones_b")])
        S.add("pool", lambda e: e.tensor_copy(out=ident_b[:], in_=ident_f[:]), reads=[k_("ident_f"), k_("ones_b")],
              writes=[k_("ident")])

        S.add("sp", lambda e: e.dma_start(out=PADT[0:P, 0:P], in_=ident_f[:]), dma=True, reads=[k_("ident")])
        with ExitStack() as pes:
            stage = pes.enter_context(nc.sbuf_tensor("pstage", [P, 4, P], F32))
            stage2 = pes.enter_context(nc.sbuf_tensor("pstage2", [P, 4, P], F32))
            c2raw = pes.enter_context(nc.sbuf_tensor("c2raw", [P, 32], F32))

            def load_T(idx, src_ap, rows, dst_ap, bank, post=None):
                st = stage if idx < 4 else stage2
                sl = idx % 4
                S.add("sp", lambda e: e.dma_start(out=st[:rows, sl, :], in_=src_ap), dma=True,
                      writes=[k_("pst", idx)])
                S.add("pe", lambda e: e.transpose(out=ps[bank][:, :rows], in_=st[:rows, sl, :],
                                                  identity=ident_f[:rows, :rows]),
                      reads=[k_("pst", idx), k_("ident")], writes=[k_("ps", bank)])
                S.add("dve", lambda e: e.tensor_copy(out=dst_ap, in_=ps[bank][:, :rows]),
                      reads=[k_("ps", bank)], writes=[k_("pc", idx)])

            load_T(0, c2_d, 32, c2raw[:], 0)
            load_T(1, bmod_d[0:128, :], 128, bm[:, 0:128], 1)
            load_T(2, bmod_d[128:144, :], 16, bm[:, 128:144], 2)
            load_T(3, ng_d, 48, ng[:], 3)
            load_T(4, fg_d, 16, fg[:], 4)
            load_T(5, convw_d, 120, cw[:], 5)
            load_T(6, mng_d, 8, mg[:], 6)
            S.add("sp", lambda e: e.dma_start(out=hg[:], in_=hng_d.rearrange("o p -> p o")), dma=True, writes=[k_("hg")])
            S.add("sp", lambda e: e.dma_start(out=sb_bc[:], in_=spb_d.partition_broadcast(P)), dma=True, writes=[k_("sb_bc")])
            spst = pes.enter_context(nc.sbuf_tensor("spst", [P, NH, P], F32))
            S.add("sp", lambda e: e.dma_start(out=spst[:], in_=spw_d.rearrange("g p q -> p g q")), dma=True, writes=[k_("spst")])
            for half in range(2):
                bank = 5 + half

                def trw(e, half=half, bank=bank):
                    ins = None
                    for j in range(4):
                        g_ = half * 4 + j
                        ins = e.transpose(out=ps[bank][:, j * P:(j + 1) * P], in_=spst[:, g_, :], identity=ident_f[:])
                    return ins
                S.add("pe", trw, reads=[k_("spst"), k_("ident")], writes=[k_("ps", bank)])
                S.add("dve", lambda e, half=half, bank=bank: e.tensor_copy(
                    out=WsT[:, half * 4:(half + 1) * 4, :], in_=ps[bank][:].rearrange("p (a j) -> p a j", a=4)),
                    reads=[k_("ps", bank)], writes=[k_("WsT", half)])
            S.add("act", lambda e: e.activation(out=sc[:].rearrange("p k r -> p r k"),
                                                in_=c2raw[:].rearrange("p (r k) -> p r k", r=2),
                                                func=AF.Silu),
                  reads=[k_("pc", 0)], writes=[k_("sc")])

            NSL = 3
            wsl = [pes.enter_context(nc.sbuf_tensor(f"wmsl{i}", [P, KT, 512], BF16)) for i in range(NSL)]
            wmod_v = wmod_d.rearrange("(k p) n -> p k n", p=P)
            MB = 7
            XT0v = XT0.rearrange("(k p) t -> p k t", p=P)
            xin = [pes.enter_context(nc.sbuf_tensor(f"xin{i}", [P, D], F32)) for i in range(2)]
            xo = [pes.enter_context(nc.sbuf_tensor(f"xo{i}", [P, KT, P], F32)) for i in range(2)]
            tbank = [0]

            def emit_T(tt):
                sl = tt % 2
                src = ctx_d[tt * P:(tt + 1) * P, :] if tt < 2 else x_d[(tt - 2) * P:(tt - 1) * P, :]
                S.add("sp", lambda e: e.dma_start(out=xin[sl][:], in_=src), dma=True, writes=[k_("xin", sl)])
                for q4 in range(4):
                    bank = tbank[0] % 7
                    tbank[0] += 1

                    def tr(e, q4=q4, bank=bank):
                        ins = None
                        for j in range(4):
                            k = q4 * 4 + j
                            ins = e.transpose(out=ps[bank][:, j * P:(j + 1) * P], in_=xin[sl][:, k * P:(k + 1) * P],
                                              identity=ident_f[:])
                        return ins
                    S.add("pe", tr, reads=[k_("xin", sl), k_("ident")], writes=[k_("ps", bank)])
                    dst = xo[sl][:, q4 * 4:(q4 + 1) * 4, :]
                    srcp = ps[bank][:].rearrange("p (j t) -> p j t", j=4)
                    if q4 % 2 == 0:
                        S.add("dve", lambda e, dst=dst, srcp=srcp: e.tensor_copy(out=dst, in_=srcp),
                              reads=[k_("ps", bank)], writes=[k_("xo", sl, q4)])
                    else:
                        S.add("act", lambda e, dst=dst, srcp=srcp: e.activation(out=dst, in_=srcp, func=AF.Copy),
                              reads=[k_("ps", bank)], writes=[k_("xo", sl, q4)])
                S.add("sp", lambda e: e.dma_start(out=XT0v[:, :, tt * P:(tt + 1) * P], in_=xo[sl][:]),
                      dma=True, reads=[k_("xo", sl, q) for q in range(4)])

            for nb in range(36):
                sl = nb % NSL
                S.add("pool", lambda e, nb=nb, sl=sl: e.dma_start(
                    out=wsl[sl][:], in_=wmod_v[:, :, nb * 512:(nb + 1) * 512]),
                    dma=True, writes=[k_("wm", sl)])

                def mm(e, nb=nb, sl=sl):
                    ins = None
                    for c in range(4):
                        n = nb * 4 + c
                        for k in range(KT):
                            ins = e.matmul(ps[MB][:, 2 * n:2 * n + 2], lhsT=wsl[sl][:, k, c * P:(c + 1) * P],
                                           rhs=sc[:, k, :], start=(k == 0), stop=(k == KT - 1))
                    return ins
                S.add("pe", mm, reads=[k_("wm", sl), k_("sc")], writes=[k_("ps", MB)])
                if nb % 2 == 1:
                    emit_T(nb // 2)
            psm = ps[MB][:, 0:288].rearrange("p (n r) -> p n r", r=2)

            def evac_mod(e):
                ins = None
                for r in range(2):
                    ins = e.tensor_tensor(out=modv[:, :, r], in0=psm[:, :, r], in1=bm[:], op=ALU.add)
                return ins
            S.add("dve", evac_mod, reads=[k_("ps", MB), k_("pc", 1), k_("pc", 2)], writes=[k_("modv")])

            def derive(e):
                ins = None
                for s in range(3):
                    for r in range(2):
                        e.scalar_tensor_tensor(out=gs[:, s, :, r], in0=modv[:, (3 * s + 1) * 16:(3 * s + 2) * 16, r],
                                               scalar=1.0, in1=ng[:, s * 16:(s + 1) * 16],
                                               op0=ALU.add, op1=ALU.mult)
                        e.tensor_copy(out=sh[:, s, :, r], in_=modv[:, (3 * s) * 16:(3 * s + 1) * 16, r])
                        ins = e.tensor_scalar(out=gate[:, s, :, r],
                                              in0=modv[:, (3 * s + 2) * 16:(3 * s + 3) * 16, r],
                                              scalar1=(1.0 if s == 1 else 0.5), scalar2=None, op0=ALU.mult)
                return ins
            S.add("dve", derive, reads=[k_("modv"), k_("pc", 3)], writes=[k_("mods")])
            S.end_phase()

        XT0v = XT0.rearrange("(k p) t -> p k t", p=P)
        XT1v = XT1.rearrange("(k p) t -> p k t", p=P)
        XT2v = XT2.rearrange("(k p) t -> p k t", p=P)
        XT3v = XT3.rearrange("(k p) t -> p k t", p=P)

        def ffn_phase(tag, XSv, XDv, w_in_ap, w_out_ap, s, blocks, precast=False):
            w_in_v = w_in_ap.rearrange("(k p) n -> p k n", p=P)
            w_out_v = w_out_ap.rearrange("(j p) n -> p j n", p=P)
            JH = [(0, 22), (22, FT)]
            WAbf, WObf = WBF[tag]
            NSQ, NTMP = 8, 4
            with ExitStack() as pes:
                xT = pes.enter_context(nc.sbuf_tensor(f"{tag}xT", [P, KT, 512], F32))
                hT = pes.enter_context(nc.sbuf_tensor(f"{tag}hT", [P, KT, 512], BF16))
                gT = pes.enter_context(nc.sbuf_tensor(f"{tag}gT", [P, FT, 512], BF16))
                wA = [pes.enter_context(nc.sbuf_tensor(f"{tag}wA{i}", [P, KT, 2, 256], BF16)) for i in range(2)]
                wO = [pes.enter_context(nc.sbuf_tensor(f"{tag}wO{i}", [P, 22, 256], BF16)) for i in range(3)]
                sq = [pes.enter_context(nc.sbuf_tensor(f"{tag}sq{i}", [P, 512], BF16)) for i in range(NSQ)]
                lnv = pes.enter_context(nc.sbuf_tensor(f"{tag}lnv", [P, 512], F32))
                rstd = pes.enter_context(nc.sbuf_tensor(f"{tag}rstd", [P, 512], F32))
                tmp = [pes.enter_context(nc.sbuf_tensor(f"{tag}tmp{i}", [P, 512], F32)) for i in range(NTMP)]
                sa = [pes.enter_context(nc.sbuf_tensor(f"{tag}sa{i}", [P, 512], F32)) for i in range(2)]
                xr = [pes.enter_context(nc.sbuf_tensor(f"{tag}xr{i}", [P, 512], F32)) for i in range(4)]
                SB = 0
                ctr = dict(wa=0, wo=0, ab=0, y=0, xr=0)

                def e_load(bi):
                    t0, n, is_ctx = blocks[bi]
                    S.add("sp", lambda e: e.dma_start(out=xT[:, :, :n], in_=XSv[:, :, t0:t0 + n]),
                          dma=True, writes=[k_("xT", k) for k in range(KT)])

                def e_stats(bi):
                    t0, n, is_ctx = blocks[bi]
                    for k in range(KT):
                        S.add("act", lambda e, k=k: e.activation(out=sq[k % NSQ][:, :n], in_=xT[:, k, :n], func=AF.Square),
                              reads=[k_("xT", k)], writes=[k_("sq", k % NSQ)])
                        S.add("pe", lambda e, k=k: e.matmul(ps[SB][:, :n], lhsT=ones_b[:], rhs=sq[k % NSQ][:, :n],
                                                            start=(k == 0), stop=(k == KT - 1)),
                              reads=[k_("sq", k % NSQ), k_("ident")], writes=[k_("ps", SB)])
                    S.add("act", lambda e: e.activation(out=lnv[:, :n], in_=ps[SB][:, :n], func=AF.Ln, scale=1.0 / D, bias=EPS),
                          reads=[k_("ps", SB)], writes=[k_("lnv")])
                    S.add("act", lambda e: e.activation(out=rstd[:, :n], in_=lnv[:, :n], func=AF.Exp, scale=-0.5),
                          reads=[k_("lnv")], writes=[k_("rstd")])

                def e_mod(bi):
                    t0, n, is_ctx = blocks[bi]
                    r = 1 if is_ctx else 0
                    for k in range(KT):
                        S.add("dve", lambda e, k=k: e.scalar_tensor_tensor(
                            out=tmp[k % NTMP][:, :n], in0=xT[:, k, :n], scalar=gs[:, s, k, r:r + 1],
                            in1=rstd[:, :n], op0=ALU.mult, op1=ALU.mult),
                            reads=[k_("xT", k), k_("rstd"), k_("mods")], writes=[k_("tmp", k % NTMP)])
                        S.add("act", lambda e, k=k: e.activation(
                            out=hT[:, k, :n], in_=tmp[k % NTMP][:, :n], func=AF.Identity, bias=sh[:, s, k, r:r + 1]),
                            reads=[k_("tmp", k % NTMP), k_("mods")], writes=[k_("hT", k)])

                def e_up(bi, mid=None):
                    t0, n, is_ctx = blocks[bi]
                    for j in range(FT):
                        if j == 24 and mid is not None:
                            mid()
                        jj = j % 2
                        if jj == 0:
                            ws = ctr["wa"] % 2
                            ctr["wa"] += 1
                            wcols = min(256, DFF - j * P)
                            si = j // 2
                            if bi == 0 and not precast:
                                S.add("pool", lambda e, ws=ws, j=j, wcols=wcols: e.dma_start(
                                    out=wA[ws][:, :, 0, 0:wcols], in_=w_in_v[:, :, j * P:j * P + wcols]),
                                    dma=True, writes=[k_("wAa", ws)])
                                S.add("pool", lambda e, ws=ws, j=j, wcols=wcols: e.dma_start(
                                    out=wA[ws][:, :, 1, 0:wcols], in_=w_in_v[:, :, DFF + j * P:DFF + j * P + wcols]),
                                    dma=True, writes=[k_("wAb", ws)])
                                S.add("sp", lambda e, ws=ws, si=si: e.dma_start(
                                    out=WAbf[si], in_=wA[ws][:].rearrange("p k t c -> p (k t c)")),
                                    dma=True, reads=[k_("wAa", ws), k_("wAb", ws)], writes=[k_("WAbf", si)])
                            else:
                                S.add("pool", lambda e, ws=ws, si=si: e.dma_start(
                                    out=wA[ws][:].rearrange("p k t c -> p (k t c)"), in_=WAbf[si]),
                                    dma=True, reads=[k_("WAbf", si)], writes=[k_("wAa", ws), k_("wAb", ws)])
                        pa = 1 + 2 * (ctr["ab"] % 2)
                        pb = pa + 1
                        ctr["ab"] += 1

                        def up(e, ws=ws, pa=pa, pb=pb, jj=jj):
                            ins = None
                            for k in range(KT):
                                e.matmul(ps[pa][:, :n], lhsT=wA[ws][:, k, 0, jj * P:(jj + 1) * P], rhs=hT[:, k, :n],
                                         start=(k == 0), stop=(k == KT - 1))
                            for k in range(KT):
                                ins = e.matmul(ps[pb][:, :n], lhsT=wA[ws][:, k, 1, jj * P:(jj + 1) * P], rhs=hT[:, k, :n],
                                               start=(k == 0), stop=(k == KT - 1))
                            return ins
                        S.add("pe", up, reads=[k_("wAa", ws), k_("wAb", ws)] + [k_("hT", k) for k in range(KT)],
                              writes=[k_("ps", pa), k_("ps", pb)])
                        ss = j % 2
                        S.add("act", lambda e, ss=ss, pa=pa: e.activation(out=sa[ss][:, :n], in_=ps[pa][:, :n], func=AF.Silu),
                              reads=[k_("ps", pa)], writes=[k_("sa", ss)])
                        S.add("dve", lambda e, ss=ss, pb=pb, j=j: e.tensor_tensor(
                            out=gT[:, j, :n], in0=sa[ss][:, :n], in1=ps[pb][:, :n], op=ALU.mult),
                            reads=[k_("sa", ss), k_("ps", pb)], writes=[k_("gT", j)])

                def e_down(bi):
                    t0, n, is_ctx = blocks[bi]
                    r = 1 if is_ctx else 0
                    for dpair in range(KT // 2):
                        yset = ctr["y"] % 2
                        ctr["y"] += 1
                        pys = (5, 6) if yset == 0 else (7, 3)
                        for jh, (j0, j1) in enumerate(JH):
                            ws = ctr["wo"] % 3
                            ctr["wo"] += 1
                            so = dpair * 2 + jh
                            if bi == 0 and not precast:
                                S.add("pool", lambda e, ws=ws, dpair=dpair, j0=j0, j1=j1: e.dma_start(
                                    out=wO[ws][:, 0:j1 - j0, :], in_=w_out_v[:, j0:j1, dpair * 256:(dpair + 1) * 256]),
                                    dma=True, writes=[k_("wO", ws)])
                            else:
                                S.add("pool", lambda e, ws=ws, so=so: e.dma_start(
                                    out=wO[ws][:].rearrange("p j c -> p (j c)"), in_=WObf[so]),
                                    dma=True, reads=[k_("WObf", so)], writes=[k_("wO", ws)])

                            def down(e, ws=ws, pys=pys, j0=j0, j1=j1):
                                ins = None
                                for dt2 in range(2):
                                    for j in range(j0, j1):
                                        ins = e.matmul(ps[pys[dt2]][:, :n], lhsT=wO[ws][:, j - j0, dt2 * P:(dt2 + 1) * P],
                                                       rhs=gT[:, j, :n], start=(j == 0), stop=(j == FT - 1),
                                                       skip_group_check=True)
                                return ins
                            S.add("pe", down, reads=[k_("wO", ws)] + [k_("gT", j) for j in range(j0, j1)],
                                  writes=[k_("ps", pys[0]), k_("ps", pys[1])])
                            if bi == 0 and not precast:
                                S.add("sp", lambda e, ws=ws, so=so: e.dma_start(
                                    out=WObf[so], in_=wO[ws][:].rearrange("p j c -> p (j c)")),
                                    dma=True, reads=[k_("wO", ws)], writes=[k_("WObf", so)])
                        for dt2 in range(2):
                            dtile = dpair * 2 + dt2
                            xs = ctr["xr"] % 4
                            ctr["xr"] += 1
                            py = pys[dt2]
                            S.add("sp", lambda e, xs=xs, dtile=dtile: e.dma_start(out=xr[xs][:, :n], in_=XSv[:, dtile, t0:t0 + n]),
                                  dma=True, writes=[k_("xr", xs)])
                            S.add("dve", lambda e, py=py, dtile=dtile, xs=xs: e.scalar_tensor_tensor(
                                out=xr[xs][:, :n], in0=ps[py][:, :n], scalar=gate[:, s, dtile, r:r + 1],
                                in1=xr[xs][:, :n], op0=ALU.mult, op1=ALU.add),
                                reads=[k_("ps", py), k_("mods"), k_("xr", xs)], writes=[k_("xr", xs)])
                            S.add("sp", lambda e, xs=xs, dtile=dtile: e.dma_start(out=XDv[:, dtile, t0:t0 + n], in_=xr[xs][:, :n]),
                                  dma=True, reads=[k_("xr", xs)])

                nbk = len(blocks)
                e_load(0)
                e_stats(0)
                e_mod(0)
                for bi in range(nbk):
                    if bi + 1 < nbk:
                        e_load(bi + 1)
                        e_up(bi, mid=lambda bi=bi: e_stats(bi + 1))
                        e_mod(bi + 1)
                    else:
                        e_up(bi)
                    e_down(bi)
                S.end_phase()

        blocks_all = [(0, 256, True)] + [(256 + 512 * i, 512, False) for i in range(4)]
        blocks_lat = [(256 + 512 * i, 512, False) for i in range(4)]
        if STAGE >= 1:
            ffn_phase("f1", XT0v, XT1v, f1in_d, f1out_d, 0, [blocks_all[1], blocks_all[0]] + blocks_all[2:])
        XFv = XT1v if STAGE >= 1 else XT0v

        def zcol(t):
            return t + 2 if t < T_CTX else t + 6

        def z_phase():
            win_v = win_d.rearrange("(k p) n -> p k n", p=P)
            with ExitStack() as zes:
                h2T = zes.enter_context(nc.sbuf_tensor("h2T", [P, KT, NTOK], BF16))
                with ExitStack() as pes:
                    xT = [pes.enter_context(nc.sbuf_tensor(f"zxT{i}", [P, KT, 512], F32)) for i in range(2)]
                    NSQ, NTMP = 8, 4
                    sq = [pes.enter_context(nc.sbuf_tensor(f"zsq{i}", [P, 512], BF16)) for i in range(NSQ)]
                    lnv = [pes.enter_context(nc.sbuf_tensor(f"zlnv{i}", [P, 512], F32)) for i in range(2)]
                    rstd = [pes.enter_context(nc.sbuf_tensor(f"zrstd{i}", [P, 512], F32)) for i in range(2)]
                    tmp = [pes.enter_context(nc.sbuf_tensor(f"ztmp{i}", [P, 512], F32)) for i in range(NTMP)]
                    s = 1
                    nbz = len(blocks_all)

                    def z_load(bi):
                        t0, n, is_ctx = blocks_all[bi]
                        sl = bi % 2
                        S.add("sp", lambda e: e.dma_start(out=xT[sl][:, :, :n], in_=XT1v[:, :, t0:t0 + n]),
                              dma=True, writes=[k_("xT", sl, k) for k in range(KT)])

                    def z_stats(bi):
                        t0, n, is_ctx = blocks_all[bi]
                        sl = bi % 2
                        SB = bi % 2
                        for k in range(KT):
                            S.add("act", lambda e, k=k: e.activation(out=sq[k % NSQ][:, :n], in_=xT[sl][:, k, :n], func=AF.Square),
                                  reads=[k_("xT", sl, k)], writes=[k_("sq", k % NSQ)])
                            S.add("pe", lambda e, k=k: e.matmul(ps[SB][:, :n], lhsT=ones_b[:], rhs=sq[k % NSQ][:, :n],
                                                                start=(k == 0), stop=(k == KT - 1)),
                                  reads=[k_("sq", k % NSQ), k_("ident")], writes=[k_("ps", SB)])
                        S.add("act", lambda e: e.activation(out=lnv[sl][:, :n], in_=ps[SB][:, :n], func=AF.Ln, scale=1.0 / D, bias=EPS),
                              reads=[k_("ps", SB)], writes=[k_("lnv", sl)])
                        S.add("act", lambda e: e.activation(out=rstd[sl][:, :n], in_=lnv[sl][:, :n], func=AF.Exp, scale=-0.5),
                              reads=[k_("lnv", sl)], writes=[k_("rstd", sl)])

                    def z_mod(bi):
                        t0, n, is_ctx = blocks_all[bi]
                        r = 1 if is_ctx else 0
                        sl = bi % 2
                        for k in range(KT):
                            S.add("dve", lambda e, k=k: e.scalar_tensor_tensor(
                                out=tmp[k % NTMP][:, :n], in0=xT[sl][:, k, :n], scalar=gs[:, s, k, r:r + 1],
                                in1=rstd[sl][:, :n], op0=ALU.mult, op1=ALU.mult),
                                reads=[k_("xT", sl, k), k_("rstd", sl), k_("mods")], writes=[k_("tmp", k % NTMP)])
                            S.add("act", lambda e, k=k: e.activation(
                                out=h2T[:, k, t0:t0 + n], in_=tmp[k % NTMP][:, :n], func=AF.Identity, bias=sh[:, s, k, r:r + 1]),
                                reads=[k_("tmp", k % NTMP), k_("mods")], writes=[k_("h2T", k, bi)])

                    z_load(0)
                    z_load(1)
                    z_stats(0)
                    for bi in range(nbz):
                        if bi + 1 < nbz:
                            z_stats(bi + 1)
                        z_mod(bi)
                        if bi + 2 < nbz:
                            z_load(bi + 2)
                    S.end_phase()

                with ExitStack() as pes:
                    wz = [pes.enter_context(nc.sbuf_tensor(f"wz{i}", [P, KT, P], BF16)) for i in range(3)]
                    zrow = [pes.enter_context(nc.sbuf_tensor(f"zrow{i}", [P, NTOK + 8], BF16)) for i in range(2)]
                    crow = [pes.enter_context(nc.sbuf_tensor(f"crow{i}", [P, NTOK], F32)) for i in range(2)]
                    sqb = [pes.enter_context(nc.sbuf_tensor(f"sqb{i}", [P, 512], BF16)) for i in range(2)]
                    lnrow = [pes.enter_context(nc.sbuf_tensor(f"lnrow{i}", [P, NTOK], F32)) for i in range(2)]
                    rinv = [pes.enter_context(nc.sbuf_tensor(f"rinv{i}", [P, 512], F32)) for i in range(2)]
                    orow = [pes.enter_context(nc.sbuf_tensor(f"zorow{i}", [P, NTOK], BF16)) for i in range(2)]
                    tst = [pes.enter_context(nc.sbuf_tensor(f"tst{i}", [P, NTT, P], BF16)) for i in range(2)]
                    dgj = [pes.enter_context(nc.sbuf_tensor(f"dgj{i}", [P, 5, P], BF16)) for i in range(2)]
                    abT = pes.enter_context(nc.sbuf_tensor("abT", [32, NTOK], F32))
                    urow = [pes.enter_context(nc.sbuf_tensor(f"urow{i}", [P, T_LAT], BF16)) for i in range(2)]
                    obrow = [pes.enter_context(nc.sbuf_tensor(f"obrow{i}", [P, T_LAT], BF16)) for i in range(2)]
                    vmt = [pes.enter_context(nc.sbuf_tensor(f"vmt{i}", [P, 4, P], BF16)) for i in range(2)]
                    t1 = [pes.enter_context(nc.sbuf_tensor(f"t1{i}", [P, 512], F32)) for i in range(2)]
                    psb = [ps[6].bitcast(BF16), ps[7].bitcast(BF16)]

                    for zs in range(2):
                        S.add("pool", lambda e, zs=zs: e.memset(zrow[zs][:], 0.0),
                              writes=[k_("zrow", zs, b) for b in range(5)])

                    jobs = [("ab", 0)]
                    for h in range(NH):
                        jobs += [("q", h), ("k", h), ("v", h)]
                    jobs += [("g", h) for h in range(NH)]
                    for g_ in range(NH):
                        jobs += [("u", g_), ("vm", g_)]
                    cnt = dict(wz=0, z=0, zs=0, cs=0, c=0, st=0, sq=0, ri=0, os=0, tr=0, ts=0, dg=0, ob=0, vm=0, t1=0)

                    def nxt(name, mod):
                        v = cnt[name] % mod
                        cnt[name] += 1
                        return v

                    def job_gen(kind, h):
                        c0 = {"q": 0, "k": 1024, "v": 2048, "g": 3072, "ab": 4096, "u": 4128, "vm": 5152}[kind] + (0 if kind == "ab" else h * P)
                        w = 32 if kind == "ab" else P
                        ws = nxt("wz", 3)
                        S.add("pool", lambda e, ws=ws, c0=c0, w=w: e.dma_start(out=wz[ws][:, :, :w], in_=win_v[:, :, c0:c0 + w]),
                              dma=True, writes=[k_("wz", ws)])
                        blks = blocks_all if kind in ("ab", "q", "k", "v") else blocks_lat
                        conv = kind in ("q", "k", "v")
                        if conv:
                            zs = nxt("zs", 2)
                            ds = nxt("dg", 2)
                            ct = {"q": 0, "k": 8, "v": 16}[kind] + h

                            def mkdg(e, ds=ds, ct=ct):
                                ins = None
                                for tap in range(5):
                                    ins = e.tensor_scalar(out=dgj[ds][:, tap, :], in0=ident_b[:],
                                                          scalar1=cw[:, tap * 24 + ct:tap * 24 + ct + 1], scalar2=None,
                                                          op0=ALU.mult)
                                return ins
                            S.add("dve", mkdg, reads=[k_("ident"), k_("pc", 5)], writes=[k_("dgj", ds)])
                        if kind in ("q", "k", "vm"):
                            cs = nxt("cs", 2)
                        if kind != "ab" and kind != "u":
                            osl = nxt("os", 2)
                        for b, (t0, n, _) in enumerate(blks):
                            bidx = b if len(blks) == 5 else b + 1
                            pz = nxt("z", 2)

                            def mm(e, ws=ws, w=w, n=n, t0=t0, pz=pz):
                                ins = None
                                for k in range(KT):
                                    ins = e.matmul(ps[pz][:w, :n], lhsT=wz[ws][:, k, :w], rhs=h2T[:, k, t0:t0 + n],
                                                   start=(k == 0), stop=(k == KT - 1))
                                return ins
                            S.add("pe", mm, reads=[k_("wz", ws)] + [k_("h2T", k, bidx) for k in range(KT)],
                                  writes=[k_("ps", pz)])
                            tl = t0 - T_CTX
                            if conv:
                                S.add("dve", lambda e, zs=zs, t0=t0, n=n, pz=pz: e.tensor_copy(
                                    out=zrow[zs][:, zcol(t0):zcol(t0) + n], in_=ps[pz][:, :n]),
                                    reads=[k_("ps", pz)], writes=[k_("zrow", zs, bidx)])
                            elif kind == "ab":
                                S.add("dve", lambda e, t0=t0, n=n, pz=pz: e.tensor_copy(out=abT[:, t0:t0 + n], in_=ps[pz][:32, :n]),
                                      reads=[k_("ps", pz)], writes=[k_("abT", bidx)])
                            elif kind == "g":
                                S.add("act", lambda e, osl=osl, tl=tl, n=n, pz=pz: e.activation(
                                    out=orow[osl][:, tl:tl + n], in_=ps[pz][:, :n], func=AF.Silu),
                                    reads=[k_("ps", pz)], writes=[k_("orow", osl, bidx)])
                            elif kind == "u":
                                S.add("act", lambda e, tl=tl, n=n, pz=pz, h=h: e.activation(
                                    out=urow[h % 2][:, tl:tl + n], in_=ps[pz][:, :n], func=AF.Gelu_apprx_tanh),
                                    reads=[k_("ps", pz)], writes=[k_("urow", h % 2, bidx)])
                            elif kind == "vm":
                                S.add("act", lambda e, cs=cs, tl=tl, n=n, pz=pz: e.activation(
                                    out=crow[cs][:, tl:tl + n], in_=ps[pz][:, :n], func=AF.Gelu_apprx_tanh),
                                    reads=[k_("ps", pz)], writes=[k_("crow", cs, bidx)])
                            yield
                        if conv:
                            for b, (t0, n, _) in enumerate(blks):
                                pc = 2 + nxt("c", 2)

                                def cv(e, zs=zs, ds=ds, t0=t0, n=n, pc=pc):
                                    ins = None
                                    base = zcol(t0) - 2
                                    for tap in range(5):
                                        ins = e.matmul(ps[pc][:, :n], lhsT=dgj[ds][:, tap, :],
                                                       rhs=zrow[zs][:, base + tap:base + tap + n],
                                                       start=(tap == 0), stop=(tap == 4))
                                    return ins
                                S.add("pe", cv, reads=[k_("dgj", ds)] + [k_("zrow", zs, bb) for bb in (b - 1, b, b + 1) if 0 <= bb < 5],
                                      writes=[k_("ps", pc)])
                                if kind == "v":
                                    S.add("act", lambda e, osl=osl, t0=t0, n=n, pc=pc: e.activation(
                                        out=orow[osl][:, t0:t0 + n], in_=ps[pc][:, :n], func=AF.Silu),
                                        reads=[k_("ps", pc)], writes=[k_("orow", osl, b)])
                                else:
                                    S.add("act", lambda e, cs=cs, t0=t0, n=n, pc=pc: e.activation(
                                        out=crow[cs][:, t0:t0 + n], in_=ps[pc][:, :n], func=AF.Silu),
                                        reads=[k_("ps", pc)], writes=[k_("crow", cs, b)])
                                yield
                        if kind in ("q", "k", "vm"):
                            for b, (t0, n, _) in enumerate(blks):
                                bidx = b if len(blks) == 5 else b + 1
                                tl = t0 if kind != "vm" else t0 - T_CTX
                                si = nxt("sq", 2)
                                pst = 4 + nxt("st", 2)
                                S.add("pool", lambda e, si=si, cs=cs, tl=tl, n=n: e.tensor_tensor(
                                    out=sqb[si][:, :n], in0=crow[cs][:, tl:tl + n], in1=crow[cs][:, tl:tl + n], op=ALU.mult),
                                    reads=[k_("crow", cs, bidx)], writes=[k_("sqb", si)])
                                S.add("pe", lambda e, si=si, n=n, pst=pst: e.matmul(ps[pst][:, :n], lhsT=ones_b[:], rhs=sqb[si][:, :n],
                                                                                    start=True, stop=True),
                                      reads=[k_("sqb", si), k_("ident")], writes=[k_("ps", pst)])
                                S.add("act", lambda e, tl=tl, n=n, pst=pst, kind=kind, cs=cs: e.activation(
                                    out=lnrow[cs][:, tl:tl + n], in_=ps[pst][:, :n], func=AF.Ln,
                                    scale=(1.0 / P if kind == "vm" else 1.0), bias=EPS),
                                    reads=[k_("ps", pst)], writes=[k_("lnrow", cs, bidx)])
                                yield
                            for b, (t0, n, _) in enumerate(blks):
                                bidx = b if len(blks) == 5 else b + 1
                                tl = t0 if kind != "vm" else t0 - T_CTX
                                ri = nxt("ri", 2)
                                qb = float(np.log(float(P) ** -0.5)) if kind == "q" else 0.0
                                S.add("act", lambda e, ri=ri, tl=tl, n=n, qb=qb, cs=cs: e.activation(
                                    out=rinv[ri][:, :n], in_=lnrow[cs][:, tl:tl + n], func=AF.Exp, scale=-0.5, bias=qb),
                                    reads=[k_("lnrow", cs, bidx)], writes=[k_("rinv", ri)])
                                if kind == "vm":
                                    S.add("dve", lambda e, ri=ri, cs=cs, osl=osl, tl=tl, n=n, h=h: e.scalar_tensor_tensor(
                                        out=orow[osl][:, tl:tl + n], in0=crow[cs][:, tl:tl + n], scalar=mg[:, h:h + 1],
                                        in1=rinv[ri][:, :n], op0=ALU.mult, op1=ALU.mult),
                                        reads=[k_("crow", cs, bidx), k_("rinv", ri), k_("pc", 6)], writes=[k_("orow", osl, bidx)])
                                else:
                                    S.add("dve", lambda e, ri=ri, cs=cs, osl=osl, tl=tl, n=n: e.tensor_tensor(
                                        out=orow[osl][:, tl:tl + n], in0=crow[cs][:, tl:tl + n], in1=rinv[ri][:, :n], op=ALU.mult),
                                        reads=[k_("crow", cs, bidx), k_("rinv", ri)], writes=[k_("orow", osl, bidx)])
                                yield
                        if kind in ("q", "k"):
                            dstT = (QTs if kind == "q" else KTs)[h * P:(h + 1) * P, :]
                            S.add("sp", lambda e, osl=osl, dstT=dstT: e.dma_start(out=dstT, in_=orow[osl][:]), dma=True,
                                  reads=[k_("orow", osl, b) for b in range(5)])
                        if kind == "g":
                            S.add("sp", lambda e, osl=osl, h=h: e.dma_start(out=SGT[h * P:(h + 1) * P, :], in_=orow[osl][:, 0:T_LAT]),
                                  dma=True, reads=[k_("orow", osl, b) for b in range(1, 5)])
                        if kind in ("k", "v"):
                            tsl = nxt("ts", 2)
                            for g8 in range(3):
                                cnt8 = 8 if g8 < 2 else 2
                                pt = nxt("tr", 2)

                                def trp(e, osl=osl, g8=g8, cnt8=cnt8, pt=pt):
                                    ins = None
                                    for j in range(cnt8):
                                        tt = g8 * 8 + j
                                        ins = e.transpose(out=psb[pt][:, j * P:(j + 1) * P], in_=orow[osl][:, tt * P:(tt + 1) * P],
                                                          identity=ident_b[:])
                                    return ins
                                S.add("pe", trp, reads=[k_("orow", osl, b) for b in range(5)] + [k_("ident")],
                                      writes=[k_("ps", 6 + pt)])
                                S.add("act", lambda e, tsl=tsl, g8=g8, cnt8=cnt8, pt=pt: e.activation(
                                    out=tst[tsl][:, g8 * 8:g8 * 8 + cnt8, :],
                                    in_=psb[pt][:, 0:cnt8 * P].rearrange("p (a j) -> p a j", a=cnt8), func=AF.Copy),
                                    reads=[k_("ps", 6 + pt)], writes=[k_("tst", tsl, g8)])
                                yield
                            dtok = (KTOK if kind == "k" else VTOK).rearrange("(tt p) c -> p tt c", p=P)[:, :, h * P:(h + 1) * P]
                            S.add("sp", lambda e, tsl=tsl, dtok=dtok: e.dma_start(out=dtok, in_=tst[tsl][:]), dma=True,
                                  reads=[k_("tst", tsl, g8) for g8 in range(3)])
                        if kind == "ab":
                            for g16 in range(2):
                                c16 = 16 if g16 == 0 else 2
                                pb = 2 + g16

                                def trab(e, g16=g16, c16=c16, pb=pb):
                                    ins = None
                                    for j in range(c16):
                                        tt = g16 * 16 + j
                                        ins = e.transpose(out=ps[pb][:, j * 32:(j + 1) * 32], in_=abT[:, tt * P:(tt + 1) * P],
                                                          identity=ident_f[:32, :32])
                                    return ins
                                S.add("pe", trab, reads=[k_("abT", b) for b in range(5)] + [k_("ident")], writes=[k_("ps", pb)])
                                S.add("dve", lambda e, g16=g16, c16=c16, pb=pb: e.tensor_copy(
                                    out=ab_tok[:, g16 * 16:g16 * 16 + c16, :],
                                    in_=ps[pb][:, 0:c16 * 32].rearrange("p (a j) -> p a j", a=c16)),
                                    reads=[k_("ps", pb)], writes=[k_("ab_tok", g16)])
                                yield
                        if kind == "vm":
                            obs = nxt("ob", 2)
                            for b in range(4):
                                pt = nxt("tr", 2)
                                vs = nxt("vm", 2)

                                def trv(e, osl=osl, b=b, pt=pt):
                                    ins = None
                                    for c4 in range(4):
                                        col = b * 512 + c4 * P
                                        ins = e.transpose(out=psb[pt][:, c4 * P:(c4 + 1) * P], in_=orow[osl][:, col:col + P],
                                                          identity=ident_b[:])
                                    return ins
                                S.add("pe", trv, reads=[k_("orow", osl, b + 1), k_("ident")], writes=[k_("ps", 6 + pt)])
                                S.add("act", lambda e, vs=vs, pt=pt: e.activation(
                                    out=vmt[vs][:], in_=psb[pt][:, 0:512].rearrange("p (a j) -> p a j", a=4), func=AF.Copy),
                                    reads=[k_("ps", 6 + pt)], writes=[k_("vmt", vs)])
                                pm = 2 + nxt("c", 2)

                                def smm(e, vs=vs, pm=pm, h=h):
                                    ins = None
                                    for c4 in range(4):
                                        ins = e.matmul(ps[pm][:, c4 * P:(c4 + 1) * P], lhsT=vmt[vs][:, c4, :], rhs=WsT[:, h, :],
                                                       start=True, stop=True)
                                    return ins
                                S.add("pe", smm, reads=[k_("vmt", vs), k_("WsT")], writes=[k_("ps", pm)])
                                ti = nxt("t1", 2)
                                S.add("dve", lambda e, ti=ti, pm=pm, h=h: e.tensor_tensor(
                                    out=t1[ti][:].rearrange("p (a j) -> p a j", a=4),
                                    in0=ps[pm][:].rearrange("p (a j) -> p a j", a=4),
                                    in1=sb_bc[:, h * P:(h + 1) * P].unsqueeze(1).to_broadcast([P, 4, P]), op=ALU.add),
                                    reads=[k_("ps", pm), k_("sb_bc")], writes=[k_("t1", ti)])
                                S.add("pool", lambda e, ti=ti, obs=obs, b=b, h=h: e.tensor_tensor(
                                    out=obrow[obs][:, b * 512:(b + 1) * 512], in0=t1[ti][:], in1=urow[h % 2][:, b * 512:(b + 1) * 512],
                                    op=ALU.mult),
                                    reads=[k_("t1", ti), k_("urow", h % 2, b + 1)], writes=[k_("obrow", obs, b)])
                                yield
                            S.add("sp", lambda e, obs=obs, h=h: e.dma_start(out=OB[h * P:(h + 1) * P, :], in_=obrow[obs][:]),
                                  dma=True, reads=[k_("obrow", obs, b) for b in range(4)])

                    pending = list(jobs)
                    active = []
                    while pending or active:
                        while pending and len(active) < 2:
                            active.append(job_gen(*pending.pop(0)))
                        for g_ in list(active):
                            try:
                                next(g_)
                            except StopIteration:
                                active.remove(g_)

                    S.end_phase()

        def precast_ffn(tag, w_in_ap, w_out_ap):
            w_in_v = w_in_ap.rearrange("(k p) n -> p k n", p=P)
            w_out_v = w_out_ap.rearrange("(j p) n -> p j n", p=P)
            WAbf, WObf = WBF[tag]
            for si in range(22):
                wcols = min(256, DFF - si * 256)
                dstv = WAbf[si].rearrange("p (k t c) -> p k t c", k=KT, t=2)
                for t in range(2):
                    S.add("pool", lambda e, dstv=dstv, t=t, si=si, wcols=wcols: e.dma_start(
                        out=dstv[:, :, t, 0:wcols], in_=w_in_v[:, :, t * DFF + si * 256:t * DFF + si * 256 + wcols]), dma=True)
            for so in range(16):
                dpair, jh = so // 2, so % 2
                j0, j1 = ((0, 22), (22, FT))[jh]
                dstv = WObf[so].rearrange("p (j c) -> p j c", c=256)
                S.add("pool", lambda e, dstv=dstv, dpair=dpair, j0=j0, j1=j1: e.dma_start(
                    out=dstv[:, 0:j1 - j0, :], in_=w_out_v[:, j0:j1, dpair * 256:(dpair + 1) * 256]), dma=True)

        def dn_phase(oacc):
            QTv = QTs.rearrange("(h p) t -> p h t", p=P)
            KTv = KTs.rearrange("(h p) t -> p h t", p=P)
            H8 = [P, NH, P]
            NHC = 4
            NCH = NH // NHC
            HC = [P, NHC, P]
            with ExitStack() as pes:
                def T_(name, shape, dt):
                    return pes.enter_context(nc.sbuf_tensor(name, shape, dt))
                mkf = T_("mkf", [P, 5, P], F32)
                mkb = T_("mkb", [P, 16, P], BF16)
                ones_f = T_("ones_f", [P, P], F32)
                al_bc = T_("al_bc", [P, 16], F32)
                dt_bc = T_("dt_bc", [P, 16], F32)
                pre = {nm: T_("pre_" + nm, [P, NTT, 16], F32) for nm in
                       ("xa", "t0", "t1", "g", "beta", "l2", "gc", "ngc", "gcl", "eg", "kbg", "kdec", "gtb", "gl0", "gl1")}
                ea = T_("pre_ea", [P, 16], F32)
                qTt = [[T_(f"qTt{d}{i}", H8, BF16) for i in range(2)] for d in range(2)]
                kTt = [[T_(f"kTt{d}{i}", H8, BF16) for i in range(2)] for d in range(2)]
                ktk = [[T_(f"ktk{d}{i}", H8, BF16) for i in range(2)] for d in range(2)]
                vtk = [[T_(f"vtk{d}{i}", H8, BF16) for i in range(2)] for d in range(2)]
                dg = [T_(f"dg{d}", H8, F32) for d in range(2)]
                EL = [T_(f"EL{d}", H8, BF16) for d in range(2)]
                ET = [T_(f"ET{d}", H8, BF16) for d in range(2)]
                EG = [T_(f"EG{d}", H8, BF16) for d in range(2)]
                Lm = [T_(f"Lm{d}", H8, BF16) for d in range(2)]
                qkT = [T_(f"qkT{d}", H8, BF16) for d in range(2)]
                Cb = [[T_(f"Cb{d}{i}", H8, BF16) for i in range(2)] for d in range(2)]
                Xb = [[T_(f"Xb{d}{i}", H8, BF16) for i in range(2)] for d in range(2)]
                Ub = [[T_(f"Ub{d}{i}", H8, BF16) for i in range(2)] for d in range(2)]
                Vb = [T_(f"Vb{d}", H8, BF16) for d in range(2)]
                vb = [T_(f"vb{d}", H8, BF16) for d in range(2)]
                kbg = [T_(f"kbg{d}", H8, BF16) for d in range(2)]
                kdc = [T_(f"kdc{d}", H8, BF16) for d in range(2)]
                qd = [T_(f"qd{d}", H8, BF16) for d in range(2)]
                u_sb = [T_(f"u_sb{d}", H8, F32) for d in range(2)]
                wT_sb = [T_(f"wT_sb{d}", H8, BF16) for d in range(2)]
                vn = [T_(f"vn{d}", H8, BF16) for d in range(2)]
                S32 = [T_(f"S32{d}", H8, F32) for d in range(2)]
                Sb = [T_(f"Sb{d}", H8, BF16) for d in range(2)]

                S.add("sp", lambda e: e.dma_start(out=mkf[:], in_=mkf_d.rearrange("m p f -> p m f")), dma=True, writes=[k_("mkf")])
                S.add("pool", lambda e: e.dma_start(out=mkb[:], in_=mkb_d.rearrange("m p f -> p m f")), dma=True, writes=[k_("mkb")])
                S.add("sp", lambda e: e.dma_start(out=al_bc[:], in_=alog_d.partition_broadcast(P)), dma=True, writes=[k_("al")])
                S.add("sp", lambda e: e.dma_start(out=dt_bc[:], in_=dtb_d.partition_broadcast(P)), dma=True, writes=[k_("dtb")])
                if STAGE >= 5:
                    precast_ffn("f2", f2in_d, f2out_d)

                def init(e):
                    e.memset(ones_f[:], 1.0)
                    for d in range(2):
                        e.memset(S32[d][:], 0.0)
                        e.memset(Sb[d][:], 0.0)
                    return e.memset(oacc[:], 0.0)
                S.add("pool", init, writes=[k_("ones_f")] + [k_(nm, d, hh) for nm in ("S32", "Sb") for d in range(2) for hh in range(NCH)] +
                      [k_("oacc", t, hh) for t in range(2, NTT) for hh in range(NCH)])

                a_ap = ab_tok[:, :, 0:16]
                b_ap = ab_tok[:, :, 16:32]
                bc18 = lambda t: t[:].unsqueeze(1).to_broadcast([P, NTT, 16])
                pk = lambda *n: [k_("pre", x) for x in n]
                S.add("dve", lambda e: e.tensor_tensor(out=pre["xa"][:], in0=a_ap, in1=bc18(dt_bc), op=ALU.add),
                      reads=[k_("dtb")], writes=pk("xa"))
                S.add("act", lambda e: e.activation(out=pre["t0"][:], in_=pre["xa"][:], func=AF.Abs),
                      reads=pk("xa"), writes=pk("t0"))
                S.add("act", lambda e: e.activation(out=pre["t0"][:], in_=pre["t0"][:], func=AF.Exp, scale=-1.0),
                      reads=pk("t0"), writes=pk("t0"))
                S.add("act", lambda e: e.activation(out=pre["t0"][:], in_=pre["t0"][:], func=AF.Ln, bias=1.0),
                      reads=pk("t0"), writes=pk("t0"))
                S.add("dve", lambda e: e.scalar_tensor_tensor(out=pre["t1"][:], in0=pre["xa"][:], scalar=0.0, in1=pre["t0"][:],
                                                              op0=ALU.max, op1=ALU.add),
                      reads=pk("xa", "t0"), writes=pk("t1"))
                S.add("act", lambda e: e.activation(out=ea[:], in_=al_bc[:], func=AF.Exp), reads=[k_("al")], writes=pk("ea"))
                S.add("dve", lambda e: e.scalar_tensor_tensor(out=pre["g"][:], in0=pre["t1"][:], scalar=-1.0, in1=bc18(ea),
                                                              op0=ALU.mult, op1=ALU.mult),
                      reads=pk("t1", "ea"), writes=pk("g"))
                S.add("act", lambda e: e.activation(out=pre["beta"][:], in_=b_ap, func=AF.Exp, scale=-1.0), writes=pk("beta"))
                S.add("dve", lambda e: e.tensor_scalar(out=pre["beta"][:], in0=pre["beta"][:], scalar1=1.0, scalar2=None, op0=ALU.add),
                      reads=pk("beta"), writes=pk("beta"))
                S.add("act", lambda e: e.activation(out=pre["l2"][:], in_=pre["beta"][:], func=AF.Ln), reads=pk("beta"), writes=pk("l2"))
                S.add("dve", lambda e: e.reciprocal(out=pre["beta"][:], in_=pre["beta"][:]), reads=pk("beta", "l2"), writes=pk("beta"))

                def cums(e):
                    for d in range(2):
                        e.matmul(ps[0][:, d * 144:(d + 1) * 144], lhsT=mkf[:, d, :], rhs=pre["g"][:, :, d * 8:(d + 1) * 8],
                                 start=True, stop=True)
                    return e.matmul(ps[1][:, 0:288], lhsT=mkf[:, 2, :], rhs=pre["g"][:], start=True, stop=True)
                S.add("pe", cums, reads=pk("g") + [k_("mkf")], writes=[k_("ps", 0), k_("ps", 1)])

                def cums_ev(e):
                    for d in range(2):
                        e.tensor_copy(out=pre["gc"][:, :, d * 8:(d + 1) * 8],
                                      in_=ps[0][:, d * 144:(d + 1) * 144].rearrange("p (t c) -> p t c", c=8))
                    return e.tensor_copy(out=pre["gtb"][:], in_=ps[1][:, 0:288].rearrange("p (t c) -> p t c", c=16))
                S.add("dve", cums_ev, reads=[k_("ps", 0), k_("ps", 1)], writes=pk("gc", "gtb"))

                def sels(e):
                    e.matmul(ps[2][:, 0:288], lhsT=mkf[:, 3, :], rhs=pre["g"][:], start=True, stop=True)
                    return e.matmul(ps[3][:, 0:288], lhsT=mkf[:, 4, :], rhs=pre["g"][:], start=True, stop=True)
                S.add("pe", sels, reads=pk("g") + [k_("mkf")], writes=[k_("ps", 2), k_("ps", 3)])

                def sels_ev(e):
                    e.activation(out=pre["gl0"][:], in_=ps[2][:, 0:288].rearrange("p (t c) -> p t c", c=16), func=AF.Exp)
                    return e.activation(out=pre["gl1"][:], in_=ps[3][:, 0:288].rearrange("p (t c) -> p t c", c=16), func=AF.Exp)
                S.add("act", sels_ev, reads=[k_("ps", 2), k_("ps", 3)], writes=pk("gl0", "gl1"))
                S.add("dve", lambda e: e.tensor_scalar(out=pre["ngc"][:], in0=pre["gc"][:], scalar1=-1.0, scalar2=None, op0=ALU.mult),
                      reads=pk("gc"), writes=pk("ngc"))
                S.add("dve", lambda e: e.tensor_tensor(out=pre["gcl"][:], in0=pre["gc"][:], in1=pre["l2"][:], op=ALU.subtract),
                      reads=pk("gc", "l2"), writes=pk("gcl"))
                S.add("act", lambda e: e.activation(out=pre["eg"][:], in_=pre["gc"][:], func=AF.Exp), reads=pk("gc"), writes=pk("eg"))
                S.add("dve", lambda e: e.tensor_tensor(out=pre["kbg"][:], in0=pre["eg"][:], in1=pre["beta"][:], op=ALU.mult),
                      reads=pk("eg", "beta"), writes=pk("kbg"))
                S.add("dve", lambda e: e.tensor_tensor(out=pre["kdec"][:], in0=pre["gtb"][:], in1=pre["gc"][:], op=ALU.subtract),
                      reads=pk("gtb", "gc"), writes=pk("kdec"))
                S.add("act", lambda e: e.activation(out=pre["kdec"][:], in_=pre["kdec"][:], func=AF.Exp), reads=pk("kdec"), writes=pk("kdec"))
                PRE_ALL = pk("gc", "ngc", "gcl", "eg", "kbg", "kdec", "beta", "gl0", "gl1")

                orders = {0: list(range(NTT)), 1: [1, 0] + list(range(NTT - 1, 1, -1))}

                def emit_loads(d, step):
                    tile = orders[d][step]
                    tc0 = tile * P
                    sl = step % 2
                    S.add("sp", lambda e: e.dma_start(out=qTt[d][sl][:], in_=QTv[:, :, tc0:tc0 + P]), dma=True, writes=[k_("qTt", d, sl)])
                    S.add("sp", lambda e: e.dma_start(out=kTt[d][sl][:], in_=KTv[:, :, tc0:tc0 + P]), dma=True, writes=[k_("kTt", d, sl)])
                    S.add("sp", lambda e: e.dma_start(out=ktk[d][sl][:].rearrange("p h j -> p (h j)"), in_=KTOK[tc0:tc0 + P, :]),
                          dma=True, writes=[k_("ktk", d, sl)])
                    S.add("sp", lambda e: e.dma_start(out=vtk[d][sl][:].rearrange("p h j -> p (h j)"), in_=VTOK[tc0:tc0 + P, :]),
                          dma=True, writes=[k_("vtk", d, sl)])

                def chain(d, hh):
                    hs = slice(hh * NHC, hh * NHC + NHC)
                    cbank = (d * NCH + hh) * 2
                    bi = [0]

                    def nb():
                        v = (cbank + (bi[0] % 2), 0)
                        bi[0] += 1
                        return v
                    K = lambda nm, *x: k_(nm, d, hh, *x)
                    identrep = ident_b[:].unsqueeze(1).to_broadcast(HC)
                    identrep_f = ident_f[:].unsqueeze(1).to_broadcast(HC)

                    def mrep(i):
                        return mkb[:, i, :].unsqueeze(1).to_broadcast(HC)

                    def bcf(nm, tile):
                        return pre[nm][:, tile, d * 8 + hh * NHC:d * 8 + hh * NHC + NHC].unsqueeze(2).to_broadcast(HC)

                    def pv(b):
                        return ps[b[0]][:, b[1]:b[1] + NHC * P].rearrange("p (h j) -> p h j", h=NHC)

                    def mm4(b, lhs_fn, rhs_fn):
                        def f(e):
                            ins = None
                            for j in range(NHC):
                                h = hh * NHC + j
                                ins = e.matmul(ps[b[0]][:, b[1] + j * P:b[1] + (j + 1) * P], lhsT=lhs_fn(h), rhs=rhs_fn(h), start=True, stop=True)
                            return ins
                        return f

                    def do_step(step):
                        tile = orders[d][step]
                        sl = step % 2
                        if hh == 0:
                            if step == 0:
                                emit_loads(d, 0)
                            if step + 1 < NTT:
                                emit_loads(d, step + 1)
                        q_, k_t, kk_, vv_ = qTt[d][sl], kTt[d][sl], ktk[d][sl], vtk[d][sl]
                        kq, kk, kkt, kv = k_("qTt", d, sl), k_("kTt", d, sl), k_("ktk", d, sl), k_("vtk", d, sl)
                        S.add("pool", lambda e, tile=tile: e.tensor_tensor(out=dg[d][:, hs, :], in0=identrep_f, in1=bcf("gc", tile), op=ALU.mult),
                              reads=PRE_ALL + [k_("ident")], writes=[K("dg")])
                        yield
                        b1 = nb()
                        S.add("pe", mm4(b1, lambda h: ones_f[:], lambda h: dg[d][:, h, :]),
                              reads=[K("dg"), k_("ones_f")], writes=[k_("ps", b1[0])])
                        yield

                        def exps(e, tile=tile, b1=b1):
                            for j in range(NHC):
                                h = hh * NHC + j
                                c = d * 8 + h
                                src_ = ps[b1[0]][:, b1[1] + j * P:b1[1] + (j + 1) * P]
                                e.activation(out=EL[d][:, h, :], in_=src_, func=AF.Exp,
                                             scale=-1.0, bias=pre["gcl"][:, tile, c:c + 1])
                                e.activation(out=ET[d][:, h, :], in_=src_, func=AF.Exp,
                                             scale=1.0, bias=pre["ngc"][:, tile, c:c + 1])
                            return e.activation(out=EG[d][:, hs, :], in_=pv(b1), func=AF.Exp)
                        S.add("act", exps, reads=[k_("ps", b1[0])] + PRE_ALL, writes=[K("EL"), K("ET"), K("EG")])
                        yield
                        b2 = nb()
                        S.add("pe", mm4(b2, lambda h: k_t[:, h, :], lambda h: k_t[:, h, :]), reads=[kk], writes=[k_("ps", b2[0])])
                        yield
                        S.add("dve", lambda e: e.scalar_tensor_tensor(out=EL[d][:, hs, :], in0=EL[d][:, hs, :], scalar=1.0, in1=mrep(0 + d),
                                                                      op0=ALU.min, op1=ALU.mult),
                              reads=[K("EL"), k_("mkb")], writes=[K("EL")])
                        yield
                        S.add("dve", lambda e, b2=b2: e.tensor_tensor(out=Lm[d][:, hs, :], in0=pv(b2), in1=EL[d][:, hs, :], op=ALU.mult),
                              reads=[K("EL"), k_("ps", b2[0])], writes=[K("Lm")])
                        yield

                        def mkC(l, slot):
                            S.add("dve" if l in (3, 5) else "pool", lambda e: e.tensor_tensor(out=Cb[d][slot][:, hs, :], in0=Lm[d][:, hs, :], in1=mrep(4 + 6 * d + l), op=ALU.mult),
                                  reads=[K("Lm"), k_("mkb")], writes=[K("Cb", slot)])
                        mkC(0, 0)
                        yield
                        b4 = nb()
                        S.add("pe", mm4(b4, lambda h: Cb[d][0][:, h, :], lambda h: ident_b[:]),
                              reads=[K("Cb", 0), k_("ident")], writes=[k_("ps", b4[0])])
                        S.add("dve", lambda e: e.tensor_tensor(out=Xb[d][0][:, hs, :], in0=identrep, in1=Cb[d][0][:, hs, :], op=ALU.subtract),
                              reads=[K("Cb", 0), k_("ident")], writes=[K("Xb", 0)])
                        mkC(1, 1)
                        yield
                        b3 = nb()
                        S.add("pe", mm4(b3, lambda h: k_t[:, h, :], lambda h: q_[:, h, :]), reads=[kk, kq], writes=[k_("ps", b3[0])])
                        S.add("dve", lambda e: e.scalar_tensor_tensor(out=ET[d][:, hs, :], in0=ET[d][:, hs, :], scalar=1.0, in1=mrep(2 + d),
                                                                      op0=ALU.min, op1=ALU.mult),
                              reads=[K("ET"), k_("mkb")], writes=[K("ET")])
                        yield
                        S.add("dve", lambda e, b3=b3: e.tensor_tensor(out=qkT[d][:, hs, :], in0=pv(b3), in1=ET[d][:, hs, :], op=ALU.mult),
                              reads=[K("ET"), k_("ps", b3[0])], writes=[K("qkT")])
                        S.add("dve", lambda e, b4=b4: e.tensor_tensor(out=Ub[d][0][:, hs, :], in0=identrep, in1=pv(b4), op=ALU.subtract),
                              reads=[k_("ps", b4[0]), k_("ident")], writes=[K("Ub", 0)])
                        yield
                        cur = 0
                        for l in range(1, 6):
                            cslot = l % 2
                            if l > 1:
                                mkC(l, cslot)
                                yield
                            ba = nb()
                            S.add("pe", mm4(ba, lambda h, cslot=cslot: Cb[d][cslot][:, h, :], lambda h, cur=cur: Ub[d][cur][:, h, :]),
                                  reads=[K("Cb", cslot), K("Ub", cur)], writes=[k_("ps", ba[0])])
                            yield
                            S.add("dve", lambda e, ba=ba: e.tensor_tensor(out=Vb[d][:, hs, :], in0=identrep, in1=pv(ba), op=ALU.subtract),
                                  reads=[k_("ps", ba[0]), k_("ident")], writes=[K("Vb")])
                            yield
                            if l < 5:
                                bb = nb()
                                S.add("pe", mm4(bb, lambda h: Vb[d][:, h, :], lambda h, cur=cur: Xb[d][cur][:, h, :]),
                                      reads=[K("Vb"), K("Xb", cur)], writes=[k_("ps", bb[0])])
                            bc_ = nb()
                            S.add("pe", mm4(bc_, lambda h, cur=cur: Xb[d][cur][:, h, :], lambda h: Vb[d][:, h, :]),
                                  reads=[K("Vb"), K("Xb", cur)], writes=[k_("ps", bc_[0])])
                            yield
                            if l < 5:
                                S.add("act", lambda e, cur=cur, bb=bb: e.activation(out=Xb[d][1 - cur][:, hs, :], in_=pv(bb), func=AF.Copy),
                                      reads=[k_("ps", bb[0])], writes=[K("Xb", 1 - cur)])
                            S.add("act", lambda e, cur=cur, bc_=bc_: e.activation(out=Ub[d][1 - cur][:, hs, :], in_=pv(bc_), func=AF.Copy),
                                  reads=[k_("ps", bc_[0])], writes=[K("Ub", 1 - cur)])
                            yield
                            cur = 1 - cur
                        Uf = Ub[d][cur]
                        UK = K("Ub", cur)
                        S.add("pool", lambda e, tile=tile: e.tensor_tensor(out=vb[d][:, hs, :], in0=vv_[:, hs, :], in1=bcf("beta", tile), op=ALU.mult),
                              reads=[kv] + PRE_ALL, writes=[K("vb")])
                        S.add("pool", lambda e, tile=tile: e.tensor_tensor(out=kbg[d][:, hs, :], in0=kk_[:, hs, :], in1=bcf("kbg", tile), op=ALU.mult),
                              reads=[kkt] + PRE_ALL, writes=[K("kbg")])
                        yield
                        bu = nb()
                        S.add("pe", mm4(bu, lambda h: Uf[:, h, :], lambda h: vb[d][:, h, :]), reads=[UK, K("vb")], writes=[k_("ps", bu[0])])
                        bw_ = nb()
                        S.add("pe", mm4(bw_, lambda h: kbg[d][:, h, :], lambda h: Uf[:, h, :]), reads=[UK, K("kbg")], writes=[k_("ps", bw_[0])])
                        S.add("pool", lambda e, tile=tile: e.tensor_tensor(out=kdc[d][:, hs, :], in0=kk_[:, hs, :], in1=bcf("kdec", tile), op=ALU.mult),
                              reads=[kkt] + PRE_ALL, writes=[K("kdc")])
                        S.add("pool", lambda e: e.tensor_tensor(out=qd[d][:, hs, :], in0=q_[:, hs, :], in1=EG[d][:, hs, :], op=ALU.mult),
                              reads=[kq, K("EG")], writes=[K("qd")])
                        yield
                        S.add("act", lambda e, bu=bu: e.activation(out=u_sb[d][:, hs, :], in_=pv(bu), func=AF.Copy),
                              reads=[k_("ps", bu[0])], writes=[K("u_sb")])
                        S.add("act", lambda e, bw_=bw_: e.activation(out=wT_sb[d][:, hs, :], in_=pv(bw_), func=AF.Copy),
                              reads=[k_("ps", bw_[0])], writes=[K("wT_sb")])
                        yield
                        for ci in ((0, 1) if d == 0 else (1, 0)):
                            c0 = 64 * ci
                            bp = nb()

                            def wS(e, c0=c0, bp=bp):
                                ins = None
                                for j in range(NHC):
                                    h = hh * NHC + j
                                    ins = e.matmul(ps[bp[0]][c0:c0 + 64, bp[1] + j * P:bp[1] + (j + 1) * P], lhsT=wT_sb[d][:, h, c0:c0 + 64],
                                                   rhs=Sb[d][:, h, :], start=True, stop=True)
                                return ins
                            S.add("pe", wS, reads=[K("wT_sb"), K("Sb")], writes=[k_("ps", bp[0])])
                            yield
                            S.add("dve", lambda e, c0=c0, bp=bp: e.tensor_tensor(
                                out=vn[d][c0:c0 + 64, hs, :], in0=u_sb[d][c0:c0 + 64, hs, :],
                                in1=ps[bp[0]][c0:c0 + 64, bp[1]:bp[1] + NHC * P].rearrange("p (h j) -> p h j", h=NHC), op=ALU.subtract),
                                reads=[k_("ps", bp[0]), K("u_sb")], writes=[K("vn")])
                            yield
                            bs_ = nb()

                            def dS(e, c0=c0, bs_=bs_):
                                ins = None
                                for j in range(NHC):
                                    h = hh * NHC + j
                                    ins = e.matmul(ps[bs_[0]][:, bs_[1] + j * P:bs_[1] + (j + 1) * P], lhsT=kdc[d][c0:c0 + 64, h, :],
                                                   rhs=vn[d][c0:c0 + 64, h, :], start=True, stop=True)
                                return ins
                            if tile >= 2:
                                bo = nb()

                                def oT(e, c0=c0, bo=bo):
                                    ins = None
                                    for j in range(NHC):
                                        h = hh * NHC + j
                                        e.matmul(ps[bo[0]][:, bo[1] + j * 64:bo[1] + (j + 1) * 64], lhsT=Sb[d][:, h, :], rhs=qd[d][:, h, c0:c0 + 64],
                                                 start=True, stop=False)
                                        ins = e.matmul(ps[bo[0]][:, bo[1] + j * 64:bo[1] + (j + 1) * 64], lhsT=vn[d][c0:c0 + 64, h, :],
                                                       rhs=qkT[d][c0:c0 + 64, h, c0:c0 + 64], start=False, stop=True)
                                    return ins
                                S.add("pe", oT, reads=[K("Sb"), K("qd"), K("vn"), K("qkT")], writes=[k_("ps", bo[0])])
                            S.add("pe", dS, reads=[K("kdc"), K("vn")], writes=[k_("ps", bs_[0])])
                            yield

                            def supd(e, ci=ci, tile=tile, bs_=bs_):
                                ins = None
                                gl = pre["gl0"] if ci == 0 else pre["gl1"]
                                for j in range(NHC):
                                    h = hh * NHC + j
                                    c = d * 8 + h
                                    ins = e.scalar_tensor_tensor(out=S32[d][:, h, :], in0=S32[d][:, h, :], scalar=gl[:, tile, c:c + 1],
                                                                 in1=ps[bs_[0]][:, bs_[1] + j * P:bs_[1] + (j + 1) * P], op0=ALU.mult, op1=ALU.add)
                                return ins
                            S.add("dve", supd, reads=[k_("ps", bs_[0]), K("S32")] + PRE_ALL, writes=[K("S32")])
                            yield
                            S.add("act", lambda e: e.activation(out=Sb[d][:, hs, :], in_=S32[d][:, hs, :], func=AF.Copy),
                                  reads=[K("S32")], writes=[K("Sb")])
                            if tile >= 2:
                                oc = (tile - 2) * P + c0
                                S.add("dve", lambda e, bo=bo, oc=oc: e.tensor_tensor(
                                    out=oacc[:, hs, oc:oc + 64], in0=oacc[:, hs, oc:oc + 64],
                                    in1=ps[bo[0]][:, bo[1]:bo[1] + NHC * 64].rearrange("p (h j) -> p h j", h=NHC), op=ALU.add),
                                    reads=[k_("ps", bo[0]), k_("oacc", tile, hh)], writes=[k_("oacc", tile, hh)])
                            yield

                    for step in range(NTT):
                        yield from do_step(step)

                gens = [chain(d, hh) for d in range(2) for hh in range(NCH)]
                for g_ in gens:
                    next(g_)
                for g_, adv in zip(gens, (0, 24, 12, 36)):
                    for _ in range(adv):
                        next(g_)
                while gens:
                    for g_ in list(gens):
                        try:
                            next(g_)
                        except StopIteration:
                            gens.remove(g_)
                if DEBUG:
                    S.add("sp", lambda e: e.dma_start(out=OACC.rearrange("(h p) t -> p h t", p=P), in_=oacc[:]), dma=True,
                          reads=[k_("oacc", t, hh) for t in range(2, NTT) for hh in range(NCH)])
                S.end_phase()

        def g_phase(oacc):
            wout_v = wout_d.rearrange("(k p) n -> p k n", p=P)
            with ExitStack() as pes:
                oaT = pes.enter_context(nc.sbuf_tensor("oaT", [P, NH, T_LAT], BF16))
                obT = pes.enter_context(nc.sbuf_tensor("obT", [P, NH, T_LAT], BF16))
                woR = pes.enter_context(nc.sbuf_tensor("woR", [P, KT // 2, KT, 256], BF16))
                NS = 4
                sgt = [pes.enter_context(nc.sbuf_tensor(f"sgt{i}", [P, 512], BF16)) for i in range(NS)]
                sqg = [pes.enter_context(nc.sbuf_tensor(f"sqg{i}", [P, 512], BF16)) for i in range(NS)]
                lng = [pes.enter_context(nc.sbuf_tensor(f"lng{i}", [P, 512], F32)) for i in range(NS)]
                tg = [pes.enter_context(nc.sbuf_tensor(f"tg{i}", [P, 512], F32)) for i in range(NS)]
                x1 = [pes.enter_context(nc.sbuf_tensor(f"x1{i}", [P, 512], F32)) for i in range(3)]
                SBK = [0, 1, 4, 5]
                PYK = [2, 3, 6, 7]
                ctr = dict(n=0, x=0)
                S.add("sp", lambda e: e.dma_start(out=obT[:], in_=OB.rearrange("(h p) t -> p h t", p=P)), dma=True, writes=[k_("obT")])
                for dp in range(KT // 2):
                    S.add("pool", lambda e, dp=dp: e.dma_start(out=woR[:, dp, :, :], in_=wout_v[:, :, dp * 256:(dp + 1) * 256]),
                          dma=True, writes=[k_("wo", dp)])

                def norm(h, b):
                    sl = ctr["n"] % NS
                    pst = SBK[ctr["n"] % 4]
                    ctr["n"] += 1
                    cs_ = slice(b * 512, (b + 1) * 512)
                    S.add("sp", lambda e: e.dma_start(out=sgt[sl][:], in_=SGT[h * P:(h + 1) * P, cs_]),
                          dma=True, writes=[k_("sgt", sl)])
                    S.add("act", lambda e: e.activation(out=sqg[sl][:], in_=oacc[:, h, cs_], func=AF.Square),
                          writes=[k_("sqg", sl)])
                    S.add("pe", lambda e: e.matmul(ps[pst][:], lhsT=ones_b[:], rhs=sqg[sl][:], start=True, stop=True),
                          reads=[k_("sqg", sl), k_("ident")], writes=[k_("ps", pst)])
                    S.add("act", lambda e: e.activation(out=lng[sl][:], in_=ps[pst][:], func=AF.Ln, scale=1.0 / P, bias=EPS),
                          reads=[k_("ps", pst)], writes=[k_("lng", sl)])
                    S.add("act", lambda e: e.activation(out=lng[sl][:], in_=lng[sl][:], func=AF.Exp, scale=-0.5),
                          reads=[k_("lng", sl)], writes=[k_("lng", sl)])
                    S.add("dve", lambda e: e.scalar_tensor_tensor(
                        out=tg[sl][:], in0=oacc[:, h, cs_], scalar=hg[:, 0:1], in1=lng[sl][:], op0=ALU.mult, op1=ALU.mult),
                        reads=[k_("lng", sl), k_("hg")], writes=[k_("tg", sl)])
                    S.add("pool", lambda e: e.tensor_tensor(out=oaT[:, h, cs_], in0=tg[sl][:], in1=sgt[sl][:], op=ALU.mult),
                          reads=[k_("tg", sl), k_("sgt", sl)], writes=[k_("oaT", h, b)])

                def proj(b, dtile):
                    cs_ = slice(b * 512, (b + 1) * 512)
                    py = PYK[ctr["x"] % 4]
                    xs = ctr["x"] % 3
                    ctr["x"] += 1
                    dp, dj = dtile // 2, dtile % 2

                    def omm(e):
                        ins = None
                        for k in range(KT):
                            src = oaT if k < 8 else obT
                            ins = e.matmul(ps[py][:], lhsT=woR[:, dp, k, dj * P:(dj + 1) * P], rhs=src[:, k % 8, cs_],
                                           start=(k == 0), stop=(k == KT - 1))
                        return ins
                    S.add("pe", omm, reads=[k_("wo", dp), k_("obT")] + [k_("oaT", h, b) for h in range(NH)], writes=[k_("ps", py)])
                    tcs = slice(T_CTX + b * 512, T_CTX + (b + 1) * 512)
                    S.add("sp", lambda e: e.dma_start(out=x1[xs][:], in_=XT1v[:, dtile, tcs]), dma=True, writes=[k_("x1", xs)])
                    S.add("dve", lambda e: e.scalar_tensor_tensor(
                        out=x1[xs][:], in0=ps[py][:], scalar=gate[:, 1, dtile, 0:1], in1=x1[xs][:], op0=ALU.mult, op1=ALU.add),
                        reads=[k_("ps", py), k_("x1", xs), k_("mods")], writes=[k_("x1", xs)])
                    S.add("sp", lambda e: e.dma_start(out=XT2v[:, dtile, tcs], in_=x1[xs][:]), dma=True, reads=[k_("x1", xs)])

                for h in range(NH):
                    norm(h, 0)
                for b in range(4):
                    for h in range(NH):
                        if b + 1 < 4:
                            norm(h, b + 1)
                        proj(b, 2 * h)
                        proj(b, 2 * h + 1)
                S.end_phase()

        if STAGE >= 2:
            z_phase()
        if STAGE >= 3:
            with ExitStack() as oes:
                oacc = oes.enter_context(nc.sbuf_tensor("oacc", [P, NH, T_LAT], BF16))
                dn_phase(oacc)
                if STAGE >= 4:
                    g_phase(oacc)
        if STAGE >= 4:
            XFv = XT2v
        if STAGE >= 5:
            ffn_phase("f2", XT2v, XT3v, f2in_d, f2out_d, 2, blocks_lat, precast=True)
            XFv = XT3v

        def out_phase(XSv, do_norm):
            with ExitStack() as pes:
                xT = [pes.enter_context(nc.sbuf_tensor(f"oxT{i}", [P, KT, 512], F32)) for i in range(2)]
                orow = [pes.enter_context(nc.sbuf_tensor(f"orow{i}", [P, D], F32)) for i in range(2)]
                sq = [pes.enter_context(nc.sbuf_tensor(f"osq{i}", [P, 512], BF16)) for i in range(4)]
                lnv = [pes.enter_context(nc.sbuf_tensor(f"olnv{i}", [P, 512], F32)) for i in range(2)]
                rstd = [pes.enter_context(nc.sbuf_tensor(f"orstd{i}", [P, 512], F32)) for i in range(2)]
                ctr = dict(o=0, b=0)
                nbk = len(blocks_lat)
                n = 512

                def e_load(bi):
                    t0 = blocks_lat[bi][0]
                    sl = bi % 2
                    S.add("sp", lambda e: e.dma_start(out=xT[sl][:, :, :n], in_=XSv[:, :, t0:t0 + n]),
                          dma=True, writes=[k_("xT", sl, k) for k in range(KT)])

                def e_stat_k(bi, k):
                    sl = bi % 2
                    SB = bi % 2
                    S.add("act", lambda e: e.activation(out=sq[k % 4][:, :n], in_=xT[sl][:, k, :n], func=AF.Square),
                          reads=[k_("xT", sl, k)], writes=[k_("sq", k % 4)])
                    S.add("pe", lambda e: e.matmul(ps[SB][:, :n], lhsT=ones_b[:], rhs=sq[k % 4][:, :n],
                                                   start=(k == 0), stop=(k == KT - 1)),
                          reads=[k_("sq", k % 4), k_("ident")], writes=[k_("ps", SB)])
                    if k == KT - 1:
                        S.add("act", lambda e: e.activation(out=lnv[sl][:, :n], in_=ps[SB][:, :n], func=AF.Ln, scale=1.0 / D, bias=EPS),
                              reads=[k_("ps", SB)], writes=[k_("lnv", sl)])
                        S.add("act", lambda e: e.activation(out=rstd[sl][:, :n], in_=lnv[sl][:, :n], func=AF.Exp, scale=-0.5),
                              reads=[k_("lnv", sl)], writes=[k_("rstd", sl)])

                def e_scale_k(bi, k):
                    sl = bi % 2
                    S.add("dve", lambda e: e.scalar_tensor_tensor(
                        out=xT[sl][:, k, :n], in0=xT[sl][:, k, :n], scalar=fg[:, k:k + 1],
                        in1=rstd[sl][:, :n], op0=ALU.mult, op1=ALU.mult),
                        reads=[k_("xT", sl, k), k_("rstd", sl), k_("pc", 4)], writes=[k_("xT", sl, k)])

                def e_tr(bi, tt):
                    t0 = blocks_lat[bi][0]
                    sl = bi % 2
                    osl = ctr["o"] % 2
                    ctr["o"] += 1
                    for q4 in range(4):
                        bank = 2 + ctr["b"] % 6
                        ctr["b"] += 1

                        def tr(e, q4=q4, bank=bank):
                            ins = None
                            for j in range(4):
                                k = q4 * 4 + j
                                ins = e.transpose(out=ps[bank][:, j * P:(j + 1) * P],
                                                  in_=xT[sl][:, k, tt * P:(tt + 1) * P], identity=ident_f[:])
                            return ins
                        S.add("pe", tr, reads=[k_("xT", sl, q4 * 4 + j) for j in range(4)] + [k_("ident")], writes=[k_("ps", bank)])
                        dst = orow[osl][:, q4 * 512:(q4 + 1) * 512]
                        if q4 % 2 == 0:
                            S.add("dve", lambda e, dst=dst, bank=bank: e.tensor_copy(out=dst, in_=ps[bank][:]),
                                  reads=[k_("ps", bank)], writes=[k_("orow", osl, q4)])
                        else:
                            S.add("act", lambda e, dst=dst, bank=bank: e.activation(out=dst, in_=ps[bank][:], func=AF.Copy),
                                  reads=[k_("ps", bank)], writes=[k_("orow", osl, q4)])
                    row0 = t0 - T_CTX + tt * P
                    S.add("sp", lambda e: e.dma_start(out=out_d[row0:row0 + P, :], in_=orow[osl][:]),
                          dma=True, reads=[k_("orow", osl, q) for q in range(4)])

                e_load(0)
                if do_norm:
                    for k in range(KT):
                        e_stat_k(0, k)
                    for k in range(KT):
                        e_scale_k(0, k)
                for bi in range(nbk):
                    nxt_ = bi + 1 < nbk
                    if nxt_:
                        e_load(bi + 1)
                    for tt in range(4):
                        e_tr(bi, tt)
                        if nxt_ and do_norm:
                            if tt < 2:
                                for k in range(tt * 8, tt * 8 + 8):
                                    e_stat_k(bi + 1, k)
                            else:
                                for k in range((tt - 2) * 8, (tt - 2) * 8 + 8):
                                    e_scale_k(bi + 1, k)
                S.end_phase()

        out_phase(XFv, STAGE >= 99)
        S.add("sp", lambda e: e.nop(), )
        S.add("act", lambda e: e.nop(), )
        S.emit()
    return nc


_CACHE = {}
_DBG = {}


def _make_masks():
    p = np.arange(P)[:, None]
    f = np.arange(P)[None, :]
    same = (p // 64) == (f // 64)
    mLf = same & (f < p)
    mLb = same & (f > p)
    mQf = same & (p <= f)
    mQb = same & (p >= f)
    mCf, mCb = [], []
    for l in range(6):
        s_ = 1 << l
        blk = (p // (2 * s_)) == (f // (2 * s_))
        mCf.append(blk & ((p % (2 * s_)) >= s_) & ((f % (2 * s_)) < s_))
        mCb.append(blk & ((p % (2 * s_)) < s_) & ((f % (2 * s_)) >= s_))
    sel0 = (p < 64) & (f >= 0)
    sel1 = (p >= 64) & (f >= 0)
    mb = np.stack([mLf, mLb, mQf, mQb] + mCf + mCb, 0).astype(np.float32)
    mf = np.stack([mQf, mQb, same, sel0, sel1], 0).astype(np.float32)
    return np.ascontiguousarray(mf), np.ascontiguousarray(mb)


def kernel(**inputs):
    import os
    f = lambda a: np.ascontiguousarray(np.asarray(a, dtype=np.float32))
    if "nc" not in _CACHE:
        _CACHE["nc"] = build_program()
    nc = _CACHE["nc"]
    ncores = int(os.environ.get("KCORES", "8"))
    x = f(inputs["x"])
    ctx = f(inputs["ctx"])
    c = f(inputs["c"])
    c_ctx = f(inputs["c_ctx"])
    mf, mb = _make_masks()
    shared = {
        "w_mod": f(inputs["w_mod"][0]),
        "b_mod": f(inputs["b_mod"][0]).reshape(144, P),
        "norm_g": f(inputs["norm_g"][0]).reshape(48, P),
        "ffn1_w_in": f(inputs["ffn1_w_in"][0]),
        "ffn1_w_out": f(inputs["ffn1_w_out"][0]),
        "w_in": f(inputs["w_in"][0]),
        "conv_w": f(inputs["conv_w"][0]).reshape(120, P),
        "a_log": f(inputs["a_log"][0]).reshape(1, 16),
        "dt_bias": f(inputs["dt_bias"][0]).reshape(1, 16),
        "head_norm_g": f(inputs["head_norm_g"][0]).reshape(1, P),
        "spatial_w": f(inputs["spatial_w"][0]),
        "spatial_b": f(inputs["spatial_b"][0]).reshape(1, NH * P),
        "mlp_norm_g": f(inputs["mlp_norm_g"][0]).reshape(8, P),
        "w_out": f(inputs["w_out"][0]),
        "ffn2_w_in": f(inputs["ffn2_w_in"][0]),
        "ffn2_w_out": f(inputs["ffn2_w_out"][0]),
        "final_g": f(inputs["final_g"]).reshape(16, P),
        "masks_f": mf,
        "masks_b": mb,
    }
    in_maps = []
    for b in range(ncores):
        m = dict(shared)
        m["x"] = x[b]
        m["ctx"] = ctx[b]
        m["c2"] = np.ascontiguousarray(np.stack([c[b], c_ctx], 0).reshape(32, P))
        in_maps.append(m)
    res = run_bass_kernel_spmd(nc, in_maps, core_ids=list(range(ncores)))
    if DEBUG:
        _DBG["res"] = res.results
    out = np.stack([np.asarray(r["out"], dtype=np.float32) for r in res.results], 0)
    if ncores < 8:
        out = np.concatenate([out, np.zeros((8 - ncores,) + out.shape[1:], np.float32)], 0)
    return out
```
